# Optimizing a Trainium2 kernel written in Bass

```python
import math
import jax, jax.numpy as jnp
from jax import lax
import numpy as np

D_MODEL = 1024
BATCH = 16
SEQ = 2048
DEPTH = 2

CTX_LEN = 256
GRID_W = 64
N_MIXERS = 2
N_HEADS = 8
HEAD_DIM = 64
V_DIM = 2 * HEAD_DIM
QK_WIDTH = N_HEADS * 2 * HEAD_DIM
V_WIDTH = N_HEADS * V_DIM
ROPE_THETA = 10000.0
ROPE_PAIRS = HEAD_DIM // 4
N_FOURIER_GROUPS = 4
FOURIER_GROUP = D_MODEL // N_FOURIER_GROUPS
D_FF = 2816
N_MOD = 9
Q_BLOCK = 128
EPS = 1e-6
N_ATTN_LAYERS = (DEPTH + N_MIXERS - 1) // N_MIXERS
N_FOURIER_LAYERS = DEPTH // N_MIXERS

kernel_name = "hybrid_diffattn_fnet_macaron_dit_block"


def rms_norm(x, g):
    x32 = x.astype(jnp.float32)
    y = x32 * lax.rsqrt(jnp.mean(x32 * x32, axis=-1, keepdims=True) + EPS)
    return (y * g.astype(jnp.float32)).astype(x.dtype)


def modulate(h, shift, scale):
    return h * (1 + scale) + shift


def swiglu(h, w_gu, w_d):
    g, u = jnp.split(h @ w_gu, 2, axis=-1)
    return (jax.nn.silu(g) * u) @ w_d


def ada_params(cond, w_mod, b_mod):
    m = jax.nn.silu(cond) @ w_mod + b_mod
    m = m.reshape(m.shape[:-1] + (N_MOD, 1, D_MODEL))
    return [m[..., k, :, :] for k in range(N_MOD)]


def axial_rope_tables(n_tokens):
    rows = n_tokens // GRID_W
    row = jnp.repeat(jnp.arange(rows, dtype=jnp.float32), GRID_W)
    col = jnp.tile(jnp.arange(GRID_W, dtype=jnp.float32), rows)
    inv_freq = ROPE_THETA ** (-(jnp.arange(ROPE_PAIRS, dtype=jnp.float32) / ROPE_PAIRS))
    ang = jnp.concatenate([row[:, None] * inv_freq, col[:, None] * inv_freq], axis=-1)
    ang = ang.reshape(n_tokens, 2, ROPE_PAIRS)
    return jnp.cos(ang), jnp.sin(ang)


def apply_axial_rope(t, cos, sin):
    shp = t.shape
    t = t.reshape(shp[:-1] + (2, 2, ROPE_PAIRS))
    t1, t2 = t[..., 0, :], t[..., 1, :]
    cs = cos.astype(t.dtype)[:, None, None]
    sn = sin.astype(t.dtype)[:, None, None]
    out = jnp.stack([t1 * cs - t2 * sn, t2 * cs + t1 * sn], axis=-2)
    return out.reshape(shp)


def diff_attn_core(q, k, v, lam):
    s = jnp.einsum('bqhcd,bkhcd->bhcqk', q.astype(jnp.float32), k.astype(jnp.float32))
    p = jax.nn.softmax(s * (1.0 / math.sqrt(HEAD_DIM)), axis=-1)
    a = p[:, :, 0] - lam * p[:, :, 1]
    return jnp.einsum('bhqk,bkhe->bqhe', a.astype(v.dtype), v)


def split_qkv(h, w_qkv):
    b, n, _ = h.shape
    qkv = h @ w_qkv
    q = qkv[..., :QK_WIDTH].reshape(b, n, N_HEADS, 2, HEAD_DIM)
    k = qkv[..., QK_WIDTH:2 * QK_WIDTH].reshape(b, n, N_HEADS, 2, HEAD_DIM)
    v = qkv[..., 2 * QK_WIDTH:].reshape(b, n, N_HEADS, V_DIM)
    return q, k, v


def diff_head_out(o, sub_g, w_o, lambda_init):
    b, n = o.shape[:2]
    o = rms_norm(o, sub_g) * (1.0 - lambda_init)
    return o.reshape(b, n, V_WIDTH) @ w_o


def diff_attention(hx, hy, w_qkv, w_o, q_g, k_g, lq1, lk1, lq2, lk2, sub_g,
                   lambda_init, cos, sin, with_ctx_queries):
    b, n, _ = hx.shape
    lam = (jnp.exp(jnp.sum(lq1.astype(jnp.float32) * lk1.astype(jnp.float32)))
           - jnp.exp(jnp.sum(lq2.astype(jnp.float32) * lk2.astype(jnp.float32)))
           + lambda_init)
    q, k, v = split_qkv(hx, w_qkv)
    q = apply_axial_rope(rms_norm(q, q_g), cos, sin)
    k = apply_axial_rope(rms_norm(k, k_g), cos, sin)
    qy, ky, vy = split_qkv(hy, w_qkv)
    ky = rms_norm(ky, k_g)
    keys = jnp.concatenate([ky, k], axis=1)
    vals = jnp.concatenate([vy, v], axis=1)
    nb = n // Q_BLOCK
    qb = q.reshape(b, nb, Q_BLOCK, N_HEADS, 2, HEAD_DIM).swapaxes(0, 1)
    ob = lax.map(lambda qq: diff_attn_core(qq, keys, vals, lam), qb)
    o = ob.swapaxes(0, 1).reshape(b, n, N_HEADS, V_DIM)
    out_x = diff_head_out(o, sub_g, w_o, lambda_init)
    out_y = None
    if with_ctx_queries:
        qy = rms_norm(qy, q_g)
        oy = diff_attn_core(qy, ky, vy, lam)
        out_y = diff_head_out(oy, sub_g, w_o, lambda_init)
    return out_x, out_y


def fourier_mix(h, w_f):
    b, n, d = h.shape
    hg = h.astype(jnp.float32).reshape(b, n, N_FOURIER_GROUPS, FOURIER_GROUP)
    f = jnp.fft.fft2(hg, axes=(1, 3), norm="ortho").real
    return f.reshape(b, n, d).astype(h.dtype) @ w_f


def setup_inputs(seed: int = 0) -> dict:
    key = jax.random.key(seed)
    ks = jax.random.split(key, 24)
    nrm = jax.random.normal
    f32 = jnp.float32
    NA, NF = N_ATTN_LAYERS, N_FOURIER_LAYERS
    return {
        "x": nrm(ks[0], (BATCH, SEQ, D_MODEL), f32),
        "c": nrm(ks[1], (BATCH, D_MODEL), f32),
        "ctx": nrm(ks[2], (BATCH, CTX_LEN, D_MODEL), f32),
        "c_ctx": nrm(ks[3], (D_MODEL,), f32),
        "norm_g": 1.0 + 0.05 * nrm(ks[4], (DEPTH, 3, D_MODEL), f32),
        "w_mod": 0.5 * nrm(ks[5], (DEPTH, D_MODEL, N_MOD * D_MODEL), f32) * D_MODEL ** -0.5,
        "b_mod": 0.01 * nrm(ks[6], (DEPTH, N_MOD * D_MODEL), f32),
        "ffn1_w_gu": nrm(ks[7], (DEPTH, D_MODEL, 2 * D_FF), f32) * D_MODEL ** -0.5,
        "ffn1_w_d": nrm(ks[8], (DEPTH, D_FF, D_MODEL), f32) * D_FF ** -0.5,
        "ffn2_w_gu": nrm(ks[9], (DEPTH, D_MODEL, 2 * D_FF), f32) * D_MODEL ** -0.5,
        "ffn2_w_d": nrm(ks[10], (DEPTH, D_FF, D_MODEL), f32) * D_FF ** -0.5,
        "attn_w_qkv": nrm(ks[11], (NA, D_MODEL, 2 * QK_WIDTH + V_WIDTH), f32) * D_MODEL ** -0.5,
        "attn_w_o": nrm(ks[12], (NA, V_WIDTH, D_MODEL), f32) * V_WIDTH ** -0.5,
        "attn_q_g": 1.0 + 0.05 * nrm(ks[13], (NA, HEAD_DIM), f32),
        "attn_k_g": 1.0 + 0.05 * nrm(ks[14], (NA, HEAD_DIM), f32),
        "attn_lam_q1": 0.1 * nrm(ks[15], (NA, HEAD_DIM), f32),
        "attn_lam_k1": 0.1 * nrm(ks[16], (NA, HEAD_DIM), f32),
        "attn_lam_q2": 0.1 * nrm(ks[17], (NA, HEAD_DIM), f32),
        "attn_lam_k2": 0.1 * nrm(ks[18], (NA, HEAD_DIM), f32),
        "attn_sub_g": 1.0 + 0.05 * nrm(ks[19], (NA, V_DIM), f32),
        "fourier_w": nrm(ks[20], (NF, D_MODEL, D_MODEL), f32) * D_MODEL ** -0.5,
    }


def reference(x, c, ctx, c_ctx, norm_g, w_mod, b_mod, ffn1_w_gu, ffn1_w_d, ffn2_w_gu,
              ffn2_w_d, attn_w_qkv, attn_w_o, attn_q_g, attn_k_g, attn_lam_q1, attn_lam_k1,
              attn_lam_q2, attn_lam_k2, attn_sub_g, fourier_w):
    n_lat = x.shape[1]
    cos, sin = axial_rope_tables(n_lat)
    y = ctx
    for i in range(DEPTH):
        kind = i % N_MIXERS
        ctx_needed_after = any(j % N_MIXERS == 0 for j in range(i + 1, DEPTH))
        ctx_in_layer = ctx_needed_after or kind == 0
        mx = ada_params(c, w_mod[i], b_mod[i])
        x = x + 0.5 * mx[2] * swiglu(modulate(rms_norm(x, norm_g[i, 0]), mx[0], mx[1]),
                                     ffn1_w_gu[i], ffn1_w_d[i])
        if ctx_in_layer:
            my = ada_params(c_ctx, w_mod[i], b_mod[i])
            y = y + 0.5 * my[2] * swiglu(modulate(rms_norm(y, norm_g[i, 0]), my[0], my[1]),
                                         ffn1_w_gu[i], ffn1_w_d[i])
        hx = modulate(rms_norm(x, norm_g[i, 1]), mx[3], mx[4])
        if kind == 0:
            a = i // N_MIXERS
            hy = modulate(rms_norm(y, norm_g[i, 1]), my[3], my[4])
            lambda_init = 0.8 - 0.6 * math.exp(-0.3 * i)
            ox, oy = diff_attention(hx, hy, attn_w_qkv[a], attn_w_o[a], attn_q_g[a],
                                    attn_k_g[a], attn_lam_q1[a], attn_lam_k1[a],
                                    attn_lam_q2[a], attn_lam_k2[a], attn_sub_g[a],
                                    lambda_init, cos, sin, ctx_needed_after)
            x = x + mx[5] * ox
            if ctx_needed_after:
                y = y + my[5] * oy
        else:
            f = i // N_MIXERS
            x = x + mx[5] * fourier_mix(hx, fourier_w[f])
            if ctx_needed_after:
                hy = modulate(rms_norm(y, norm_g[i, 1]), my[3], my[4])
                y = y + my[5] * fourier_mix(hy, fourier_w[f])
        x = x + 0.5 * mx[8] * swiglu(modulate(rms_norm(x, norm_g[i, 2]), mx[6], mx[7]),
                                     ffn2_w_gu[i], ffn2_w_d[i])
        if ctx_needed_after:
            y = y + 0.5 * my[8] * swiglu(modulate(rms_norm(y, norm_g[i, 2]), my[6], my[7]),
                                         ffn2_w_gu[i], ffn2_w_d[i])
    return x
```

```python
import math
from contextlib import ExitStack
import numpy as np
import ml_dtypes
import concourse.bass as bass
import concourse.mybir as mybir
from concourse.bass_utils import run_bass_kernel_spmd

F32 = mybir.dt.float32
BF16 = mybir.dt.bfloat16
AF = mybir.ActivationFunctionType
ALU = mybir.AluOpType
AX = mybir.AxisListType

D = 1024
NCH = 8
NH = 8
CTX = 256
EPS = 1e-6
NSLOT = 8
STRICT_SAME_ENGINE = True
LAMBDA_INIT0 = 0.8 - 0.6 * math.exp(-0.3 * 0)


class Sched:
    def __init__(self):
        self.threads = ["pe", "act", "dve", "pool", "sp"]
        self.dma_queues = ["sp", "pool"]
        self.counters = list(self.threads)
        for q in self.dma_queues:
            for s in range(NSLOT):
                self.counters.append(("dma", q, s))
        self.cidx = {c: i for i, c in enumerate(self.counters)}
        self.ops = []
        self.last_writer = {}
        self.readers = {}
        self.dma_count = {q: 0 for q in self.dma_queues}
        self.slot_last = {}
        self.last_on = {}
        self.barrier_deps = {}

    def barrier(self):
        deps = set(self.last_on.values())
        for t in self.threads:
            self.barrier_deps.setdefault(t, set()).update(deps)

    def add(self, thread, fn, reads=(), writes=(), dma=False):
        i = len(self.ops)
        raw, other = set(), set()
        for r in reads:
            w = self.last_writer.get(r)
            if w is not None:
                raw.add(w)
        for k in writes:
            w = self.last_writer.get(k)
            if w is not None:
                other.add(w)
            rd = self.readers.get(k)
            if rd:
                other.update(rd.values())
        bd = self.barrier_deps.pop(thread, None)
        if bd:
            raw.update(bd)
        if dma:
            n = self.dma_count[thread]
            self.dma_count[thread] = n + 1
            counter = self.cidx[("dma", thread, n % NSLOT)]
            prev = self.slot_last.get(counter)
            if prev is not None:
                other.add(prev)
            self.slot_last[counter] = i
        else:
            counter = self.cidx[thread]
        self.last_on[counter] = i
        for r in reads:
            self.readers.setdefault(r, {})[counter] = i
        for k in writes:
            self.last_writer[k] = i
            self.readers[k] = {}
        raw.discard(i)
        other.discard(i)
        other -= raw
        self.ops.append(dict(thread=thread, fn=fn, dma=dma, counter=counter,
                             raw=sorted(raw), other=sorted(other)))
        return i

    def finalize(self):
        nC = len(self.counters)
        K = {t: [-1] * nC for t in self.threads}
        pos = {t: 0 for t in self.threads}
        ops = self.ops
        VC = [None] * len(ops)
        signaled = [False] * len(ops)
        for i, op in enumerate(ops):
            t = op["thread"]
            k = K[t]
            waits = []
            op["pos"] = pos[t]
            pos[t] += 1
            for kind, deps in (("raw", op["raw"]), ("other", op["other"])):
                for j in deps:
                    oj = ops[j]
                    if k[oj["counter"]] >= j:
                        continue
                    if (not oj["dma"]) and (not op["dma"]) and oj["thread"] == t:
                        if t == "pe" or kind == "other":
                            continue
                        if (not STRICT_SAME_ENGINE) and op["pos"] - oj["pos"] >= 3:
                            continue
                    waits.append(j)
                    signaled[j] = True
                    vj = VC[j]
                    for c in range(nC):
                        if vj[c] > k[c]:
                            k[c] = vj[c]
            op["waits"] = waits
            v = list(k)
            v[op["counter"]] = i
            VC[i] = v
            if op["dma"]:
                signaled[i] = True
        cnt = [0] * nC
        for i, op in enumerate(ops):
            if signaled[i]:
                c = op["counter"]
                cnt[c] += 16 if op["dma"] else 1
                op["sigval"] = cnt[c]
            else:
                op["sigval"] = None

    def emit(self, sems, thread, eng):
        ops = self.ops
        for op in ops:
            if op["thread"] != thread:
                continue
            for j in op["waits"]:
                oj = ops[j]
                eng.wait_ge(sems[oj["counter"]], oj["sigval"])
            ins = op["fn"](eng)
            if op["sigval"] is not None:
                ins.then_inc(sems[op["counter"]], 16 if op["dma"] else 1)


def _bf(a):
    return np.ascontiguousarray(a).astype(ml_dtypes.bfloat16)


def _partner_perm():
    p = np.arange(128)
    half = (p // 16) % 2
    return np.where(half == 0, p + 16, p - 16)


def _rope_tables(seq):
    grid_w = 64
    pairs = 16
    t = np.arange(seq)
    row = (t // grid_w).astype(np.float32)
    col = (t % grid_w).astype(np.float32)
    inv_freq = (10000.0 ** (-(np.arange(pairs, dtype=np.float32) / pairs))).astype(np.float32)
    p = np.arange(128)
    d = p % 64
    axis = d // 32
    half = (d // 16) % 2
    pair = d % 16
    pos = np.where(axis[:, None] == 0, row[None, :], col[None, :]).astype(np.float32)
    ang = (pos * inv_freq[pair][:, None]).astype(np.float32)
    cos = np.cos(ang).astype(np.float32)
    sin = np.sin(ang).astype(np.float32)
    sgn = np.where(half == 0, -1.0, 1.0).astype(np.float32)[:, None]
    return np.ascontiguousarray(cos), np.ascontiguousarray(sin * sgn)


def _dft_tables(seq):
    C = 256
    n = np.arange(seq, dtype=np.int64)
    m = (n[:, None] * n[None, :]) % seq
    ang = 2.0 * np.pi * m.astype(np.float64) / seq
    scale = 1.0 / math.sqrt(seq * C)
    cn = (np.cos(ang) * scale)
    sn = (-np.sin(ang) * scale)
    nkt = seq // 256
    nnt = seq // 128
    tab = np.stack([cn, sn]).reshape(2, nnt, 128, nkt, 256).transpose(0, 3, 2, 1, 4)
    c = np.arange(C, dtype=np.int64)
    mc = (c[:, None] * c[None, :]) % C
    angc = 2.0 * np.pi * mc.astype(np.float64) / C
    tc = np.stack([np.cos(angc), np.sin(angc)]).reshape(2, 2, 128, 256).transpose(0, 2, 1, 3)
    return _bf(tab), _bf(tc)


def build_program(SEQ, DFF, NB):
    NT = SEQ // 512
    TILES = [(i * 512, 512) for i in range(NT)] + [(SEQ, CTX)]
    NTOK = SEQ + CTX
    NKC = NTOK // 128
    NFC = DFF // 128
    NG = NFC // 2
    NQT = SEQ // 256

    nc = bass.Bass("TRN2", target_bir_lowering=False)
    dr = {}

    def din(name, shape, dt=F32):
        dr[name] = nc.dram_tensor(name, list(shape), dt, kind="ExternalInput").ap()
        return dr[name]

    x_d = din("x", [NB, SEQ, D])
    ctx_d = din("ctx", [NB, CTX, D])
    condT_d = din("condT", [128, NCH, 3])
    gT_d = din("gT", [128, 2, 3, NCH])
    bT_d = din("bT", [128, 2, 72])
    wmod_d = din("wmod", [2, 9, 128, NCH, 1024])
    wgu_d = din("wgu", [2, 2, NG, 128, NCH, 512])
    wd_d = din("wd", [2, 2, NG, 128, 2, 1024])
    wattn_d = din("wattn", [NH, 128, NCH, 5, 128])
    wo_d = din("wo", [NH // 2, 128, 2, 1024])
    wf_d = din("wf", [2, 128, 4, 1024])
    gcols_d = din("gcols", [128, 5])
    lamv_d = din("lamv", [4, 64])
    ident_d = din("ident", [128, 128])
    bones_d = din("bones", [128, 128])
    cos_d = din("ropecos", [128, SEQ])
    sin_d = din("ropesin", [128, SEQ])
    dftn_d = din("dftn", [2, SEQ // 256, 128, SEQ // 128, 256], BF16)
    dftc_d = din("dftc", [2, 128, 2, 256], BF16)
    out_d = nc.dram_tensor("out", [NB, SEQ, D], F32, kind="ExternalOutput").ap()

    S = Sched()
    add = S.add
    es = ExitStack()
    with es:
        xs = es.enter_context(nc.sbuf_tensor("xs", [128, NCH, SEQ], F32))
        hb = es.enter_context(nc.sbuf_tensor("hb", [128, NCH, SEQ], BF16))
        SCR_K = 82
        scr = es.enter_context(nc.sbuf_tensor("scr", [128, SCR_K * 512], BF16))
        pers = es.enter_context(nc.sbuf_tensor("pers", [128, 1600], F32))
        persb = es.enter_context(nc.sbuf_tensor("persb", [128, 512], BF16))
        psall = es.enter_context(nc.psum_tensor("psall", [128, 4096], F32))
        sems = [es.enter_context(nc.semaphore(f"s{i}")) for i in range(len(S.counters))]

        def bank(i, w=512, off=0):
            return psall[:, i * 512 + off: i * 512 + off + w]

        def carve(off_b, shape, dt):
            esz = 4 if dt == F32 else 2
            n = int(np.prod(shape[1:]))
            assert off_b % 4 == 0
            assert off_b + n * esz <= SCR_K * 1024, (off_b, shape)
            a = scr[:, off_b // 2: off_b // 2 + n * esz // 2]
            if dt == F32:
                a = a.bitcast(F32)
            if len(shape) == 3:
                a = a.rearrange("p (a b) -> p a b", b=shape[2])
            elif len(shape) == 4:
                a = a.rearrange("p (a b c) -> p a b c", b=shape[2], c=shape[3])
            return a

        KB = 1024
        po = [0]

        def ptake(n):
            a = pers[:, po[0]: po[0] + n]
            po[0] += n
            assert po[0] <= 1600
            return a
        ident = ptake(128)
        modsT = ptake(2 * 72 * 3).rearrange("p (l n j) -> p l n j", l=2, n=72)
        Amod = ptake(2 * 3 * 8 * 3).rearrange("p (l s c j) -> p l s c j", l=2, s=3, c=8)
        Gh = ptake(2 * 3 * 8 * 3).rearrange("p (l s c j) -> p l s c j", l=2, s=3, c=8)
        gT = ptake(2 * 3 * 8).rearrange("p (l s c) -> p l s c", l=2, s=3)
        bT = ptake(2 * 72).rearrange("p (l n) -> p l n", l=2)
        condT = ptake(24).rearrange("p (c j) -> p c j", j=3)
        gcols = ptake(5)
        lamv = ptake(256).rearrange("p (a b) -> p a b", a=4)
        lamt = ptake(128).rearrange("p (a b) -> p a b", a=2)
        lams = ptake(8)
        ones = persb[:, 0:128]
        bones = persb[:, 128:256]
        scT = persb[:, 256:280].rearrange("p (c j) -> p c j", j=3)

        wgu_s = [carve(i * 8 * KB, [128, NCH, 512], BF16) for i in range(2)]
        wd_s = [carve(16 * KB + i * 4 * KB, [128, 2, 1024], BF16) for i in range(2)]
        aT_s = [carve(24 * KB + i * 2 * KB, [128, 2, 512], BF16) for i in range(2)]
        sg_s = [carve(28 * KB + i * 2 * KB, [128, 512], F32) for i in range(2)]
        sqb_s = [carve(32 * KB + i * KB, [128, 512], BF16) for i in range(2)]
        sd_f = carve(34 * KB, [128, 512], F32)
        rstd_f = carve(36 * KB, [128, 512], F32)
        tmp_s = [carve(38 * KB + i * 2 * KB, [128, 512], F32) for i in range(2)]
        stage = [carve(42 * KB + i * 4 * KB, [128, 1024], F32) for i in range(2)]
        ys = carve(50 * KB, [128, NCH, CTX], F32)
        hy = carve(58 * KB, [128, NCH, CTX], BF16)
        wm_s = [carve(i * 16 * KB, [128, NCH, 1024], BF16) for i in range(2)]
        wh_s = [carve(i * 10 * KB, [128, NCH, 5, 128], BF16) for i in range(2)]
        wo_s = carve(20 * KB, [128, 2, 1024], BF16)
        qT = carve(24 * KB, [128, 2048], BF16)
        kT = carve(28 * KB, [128, 2304], BF16)
        v_sb = carve(28 * KB + 4608, [128, 18, 128], BF16)
        onT = carve(37 * KB, [128, 2, 2048], BF16)
        P_s = [carve(45 * KB + i * 2 * KB, [128, 2, 2, 256], BF16) for i in range(2)]
        sqa = carve(49 * KB, [128, 512], BF16)
        T_a = [carve(50 * KB + i * 2 * KB, [128, 512], F32) for i in range(4)]
        cos_t = carve(62 * KB, [128, 2048], F32)
        sin_t = carve(70 * KB, [128, 2048], F32)
        T_a += [carve(78 * KB + i * 2 * KB, [128, 512], F32) for i in range(2)]
        NNT_ = SEQ // 128
        U_c = carve(0, [128, NNT_, 512], BF16)
        U_s = carve(16 * KB, [128, NNT_, 512], BF16)
        dft_s = [carve(32 * KB + i * 16 * KB, [128, 2, NNT_, 256], BF16) for i in range(2)]
        wf_s = carve(64 * KB, [128, 4, 1024], BF16)
        fT_s = [carve(72 * KB + i * 2 * KB, [128, 4, 256], BF16) for i in range(2)]
        dftc_t = carve(76 * KB, [128, 2, 2, 256], BF16)

        def xv(c, t0, w):
            if t0 < SEQ:
                return xs[:, c, t0:t0 + w]
            return ys[:, c, t0 - SEQ:t0 - SEQ + w]

        def hv(c, t0, w):
            if t0 < SEQ:
                return hb[:, c, t0:t0 + w]
            return hy[:, c, t0 - SEQ:t0 - SEQ + w]

        def tix(t0):
            return t0 // 512

        cnt = {"cp": 0}

        def copy_alt(out, in_, reads, writes):
            cnt["cp"] += 1
            if cnt["cp"] % 2:
                add("act", lambda e: e.activation(out, in_, AF.Copy), reads=reads, writes=writes)
            else:
                add("dve", lambda e: e.tensor_copy(out, in_), reads=reads, writes=writes)

        def load_const(dst, src, key):
            add("sp", lambda e: e.dma_start(out=dst, in_=src), writes=[key], dma=True)
        load_const(ident, ident_d, "ident")
        load_const(condT.rearrange("p c j -> p (c j)"), condT_d.rearrange("p c j -> p (c j)"), "condT")
        load_const(gT.rearrange("p l s c -> p (l s c)"), gT_d.rearrange("p l s c -> p (l s c)"), "gT")
        load_const(bT.rearrange("p l n -> p (l n)"), bT_d.rearrange("p l n -> p (l n)"), "bT")
        load_const(gcols, gcols_d, "gcols")
        for a in range(4):
            load_const(lamv[:, a, :], lamv_d[a].partition_broadcast(128), "lamv")
        add("pool", lambda e: e.memset(ones, 1.0), writes=["ones"])
        add("pool", lambda e: e.dma_start(out=bones, in_=bones_d), writes=["bones"], dma=True)

        add("dve", lambda e: e.tensor_tensor(lamt[:, 0, :], lamv[:, 0, :], lamv[:, 1, :], op=ALU.mult), reads=["lamv"], writes=["lamt0"])
        add("dve", lambda e: e.tensor_tensor(lamt[:, 1, :], lamv[:, 2, :], lamv[:, 3, :], op=ALU.mult), reads=["lamv"], writes=["lamt1"])
        add("dve", lambda e: e.tensor_reduce(lams[:, 0:1], lamt[:, 0, :], axis=AX.X, op=ALU.add), reads=["lamt0"], writes=["lams0"])
        add("dve", lambda e: e.tensor_reduce(lams[:, 1:2], lamt[:, 1, :], axis=AX.X, op=ALU.add), reads=["lamt1"], writes=["lams1"])
        add("act", lambda e: e.activation(lams[:, 2:4], lams[:, 0:2], AF.Exp), reads=["lams0", "lams1"], writes=["lams2"])
        add("dve", lambda e: e.tensor_tensor(lams[:, 4:5], lams[:, 3:4], lams[:, 2:3], op=ALU.subtract), reads=["lams2"], writes=["lams4a"])
        add("dve", lambda e: e.tensor_scalar(lams[:, 5:6], lams[:, 4:5], -LAMBDA_INIT0, 1.0, op0=ALU.add, op1=ALU.mult), reads=["lams4a"], writes=["neglam"])
        neglam = lams[:, 5:6]
        add("dve", lambda e: e.tensor_scalar(lams[:, 6:7], gcols[:, 4:5], 1.0 - LAMBDA_INIT0, 0.0, op0=ALU.mult, op1=ALU.add), reads=["gcols"], writes=["subg08"])
        subg08 = lams[:, 6:7]

        add("act", lambda e: e.activation(scT.rearrange("p c j -> p (c j)"), condT.rearrange("p c j -> p (c j)"), AF.Silu),
            reads=["condT"], writes=["scT"])
        for l in range(2):
            for m in range(9):
                sl = (l * 9 + m) % 2
                wm = wm_s[sl]
                add("pool", lambda e, wm=wm, l=l, m=m: e.dma_start(
                    out=wm.rearrange("p c n -> p (c n)"), in_=wmod_d[l, m].rearrange("p c n -> p (c n)")),
                    writes=[("wm", sl)], dma=True)
                pb = bank(sl)
                for nn in range(8):
                    for kc in range(NCH):
                        add("pe", lambda e, wm=wm, pb=pb, nn=nn, kc=kc: e.matmul(
                            pb[:, nn * 3:nn * 3 + 3], wm[:, kc, nn * 128:(nn + 1) * 128], scT[:, kc, :],
                            start=(kc == 0), stop=(kc == NCH - 1)),
                            reads=[("wm", sl), "scT"], writes=[("ps", sl)])
                for j in range(3):
                    add("dve", lambda e, pb=pb, l=l, m=m, j=j: e.tensor_tensor(
                        modsT[:, l, m * 8:(m + 1) * 8, j], pb[:, 0:24].rearrange("p (n j) -> p n j", j=3)[:, :, j],
                        bT[:, l, m * 8:(m + 1) * 8], op=ALU.add),
                        reads=[("ps", sl), "bT"], writes=[("mods", l, m)])
            for s in range(3):
                for j in range(3):
                    add("dve", lambda e, l=l, s=s, j=j: e.scalar_tensor_tensor(
                        Amod[:, l, s, :, j], modsT[:, l, (3 * s + 1) * 8:(3 * s + 2) * 8, j], 1.0, gT[:, l, s, :],
                        op0=ALU.add, op1=ALU.mult),
                        reads=[("mods", l, 3 * s + 1), "gT"], writes=[("Amod", l, s)])
                    add("dve", lambda e, l=l, s=s, j=j: e.tensor_scalar(
                        Gh[:, l, s, :, j], modsT[:, l, (3 * s + 2) * 8:(3 * s + 3) * 8, j],
                        (1.0 if s == 1 else 0.5), 0.0, op0=ALU.mult, op1=ALU.add),
                        reads=[("mods", l, 3 * s + 2)], writes=[("Gh", l, s)])

        def Acol(l, s, c, j):
            return Amod[:, l, s, c, j:j + 1]

        def Bcol(l, s, c, j):
            return modsT[:, l, (3 * s) * 8 + c, j:j + 1]

        def Gcol(l, s, c, j):
            return Gh[:, l, s, c, j:j + 1]

        S.barrier()

        def load_stream(bi):
            chunks = [(x_d[bi, t0:t0 + 128, :], t0) for t0 in range(0, SEQ, 128)]
            chunks += [(ctx_d[bi, t0:t0 + 128, :], SEQ + t0) for t0 in range(0, CTX, 128)]
            for ci, (src, t0) in enumerate(chunks):
                st = stage[ci % 2]
                add("sp", lambda e, st=st, src=src: e.dma_start(out=st, in_=src), writes=[("stage", ci % 2)], dma=True)
                for half in range(2):
                    b = (ci % 2) * 2 + half
                    for cc in range(4):
                        c = half * 4 + cc
                        add("pe", lambda e, b=b, cc=cc, c=c, st=st: e.transpose(
                            bank(b, 128, cc * 128), st[:, c * 128:(c + 1) * 128], ident),
                            reads=[("stage", ci % 2), "ident"], writes=[("ps", b)])
                    if t0 < SEQ:
                        dst = xs[:, half * 4:half * 4 + 4, t0:t0 + 128]
                    else:
                        dst = ys[:, half * 4:half * 4 + 4, t0 - SEQ:t0 - SEQ + 128]
                    copy_alt(dst, bank(b).rearrange("p (c t) -> p c t", c=4), reads=[("ps", b)],
                             writes=[("x", half * 4 + cc, tix(t0)) for cc in range(4)])

        def store_stream(bi):
            for ci, t0 in enumerate(range(0, SEQ, 128)):
                st = stage[ci % 2]
                for half in range(2):
                    b = (ci % 2) * 2 + half
                    for cc in range(4):
                        c = half * 4 + cc
                        add("pe", lambda e, b=b, cc=cc, c=c, t0=t0: e.transpose(
                            bank(b, 128, cc * 128), xs[:, c, t0:t0 + 128], ident),
                            reads=[("x", c, tix(t0)), "ident"], writes=[("ps", b)])
                    copy_alt(st[:, half * 512:(half + 1) * 512], bank(b), reads=[("ps", b)],
                             writes=[("stage", ci % 2, half)])
                add("sp", lambda e, st=st, t0=t0: e.dma_start(out=out_d[bi, t0:t0 + 128, :], in_=st),
                    reads=[("stage", ci % 2, 0), ("stage", ci % 2, 1)], writes=[("stage", ci % 2), ("out", bi, ci)], dma=True)

        nrm = {"i": 0}

        def norm_mod(l, s, j_lat, tiles):
            for (t0, w) in tiles:
                j = j_lat if t0 < SEQ else 2
                ti = tix(t0)
                nrm["i"] += 1
                pb = 6 + nrm["i"] % 2
                for c in range(NCH):
                    sq = sqb_s[c % 2]
                    add("act", lambda e, sq=sq, c=c, t0=t0, w=w: e.activation(sq[:, 0:w], xv(c, t0, w), AF.Square),
                        reads=[("x", c, ti)], writes=[("sqb", c % 2)])
                    add("pe", lambda e, sq=sq, c=c, w=w, pb=pb: e.matmul(
                        bank(pb, w), ones, sq[:, 0:w], start=(c == 0), stop=(c == NCH - 1)),
                        reads=[("sqb", c % 2), "ones"], writes=[("ps", pb)])
                add("act", lambda e, w=w, pb=pb: e.activation(sd_f[:, 0:w], bank(pb, w), AF.Sqrt, scale=1.0 / D, bias=EPS),
                    reads=[("ps", pb)], writes=["sd"])
                add("dve", lambda e, w=w: e.reciprocal(rstd_f[:, 0:w], sd_f[:, 0:w]), reads=["sd"], writes=["rstd"])
                for c in range(NCH):
                    tm = tmp_s[c % 2]
                    add("dve", lambda e, tm=tm, c=c, t0=t0, w=w: e.tensor_tensor(
                        tm[:, 0:w], xv(c, t0, w), rstd_f[:, 0:w], op=ALU.mult),
                        reads=[("x", c, ti), "rstd"], writes=[("tmp", c % 2)])
                    add("act", lambda e, tm=tm, c=c, t0=t0, w=w, j=j: e.activation(
                        hv(c, t0, w), tm[:, 0:w], AF.Identity, bias=Bcol(l, s, c, j), scale=Acol(l, s, c, j)),
                        reads=[("tmp", c % 2), ("Amod", l, s), ("mods", l, 3 * s)], writes=[("h", c, ti)])

        ffc = {"g": 0, "o": 0, "a": 0}

        def ffn(l, which, j_lat, tiles, next_norm=None):
            s = 0 if which == 0 else 2
            for g in range(NG):
                gi = ffc["g"]
                ffc["g"] += 1
                sl = gi % 2
                wgu, wd = wgu_s[sl], wd_s[sl]
                add("pool", lambda e, wgu=wgu, g=g: e.dma_start(
                    out=wgu.rearrange("p c n -> p (c n)"), in_=wgu_d[l, which, g].rearrange("p c n -> p (c n)")),
                    writes=[("wgu", sl)], dma=True)
                add("pool", lambda e, wd=wd, g=g: e.dma_start(
                    out=wd.rearrange("p c n -> p (c n)"), in_=wd_d[l, which, g].rearrange("p c n -> p (c n)")),
                    writes=[("wd", sl)], dma=True)
                for (t0, w) in tiles:
                    j = j_lat if t0 < SEQ else 2
                    ti = tix(t0)
                    ffc["a"] += 1
                    asl = ffc["a"] % 2
                    aT = aT_s[asl]
                    for fj in range(2):
                        gb, ub = fj, 2 + fj
                        for kc in range(NCH):
                            add("pe", lambda e, kc=kc, fj=fj, gb=gb, t0=t0, w=w, wgu=wgu: e.matmul(
                                bank(gb, w), wgu[:, kc, fj * 128:(fj + 1) * 128], hv(kc, t0, w),
                                start=(kc == 0), stop=(kc == NCH - 1)),
                                reads=[("wgu", sl), ("h", kc, ti)], writes=[("ps", gb)])
                        for kc in range(NCH):
                            add("pe", lambda e, kc=kc, fj=fj, ub=ub, t0=t0, w=w, wgu=wgu: e.matmul(
                                bank(ub, w), wgu[:, kc, 256 + fj * 128:256 + (fj + 1) * 128], hv(kc, t0, w),
                                start=(kc == 0), stop=(kc == NCH - 1)),
                                reads=[("wgu", sl), ("h", kc, ti)], writes=[("ps", ub)])
                        sg = sg_s[fj]
                        add("act", lambda e, sg=sg, gb=gb, w=w: e.activation(sg[:, 0:w], bank(gb, w), AF.Silu),
                            reads=[("ps", gb)], writes=[("sg", fj)])
                        add("dve", lambda e, sg=sg, ub=ub, w=w, aT=aT, fj=fj: e.tensor_tensor(
                            aT[:, fj, 0:w], sg[:, 0:w], bank(ub, w), op=ALU.mult),
                            reads=[("sg", fj), ("ps", ub)], writes=[("aT", asl, fj)])
                    for d in range(NCH):
                        ob = 4 + ffc["o"] % 4
                        ffc["o"] += 1
                        for fj in range(2):
                            add("pe", lambda e, fj=fj, d=d, ob=ob, w=w, aT=aT, wd=wd: e.matmul(
                                bank(ob, w), wd[:, fj, d * 128:(d + 1) * 128], aT[:, fj, 0:w],
                                start=(fj == 0), stop=(fj == 1)),
                                reads=[("wd", sl), ("aT", asl, fj)], writes=[("ps", ob)])
                        add("dve", lambda e, d=d, ob=ob, t0=t0, w=w, j=j: e.scalar_tensor_tensor(
                            xv(d, t0, w), bank(ob, w), Gcol(l, s, d, j), xv(d, t0, w), op0=ALU.mult, op1=ALU.add),
                            reads=[("ps", ob), ("x", d, ti), ("Gh", l, s)], writes=[("x", d, ti)])
                    if next_norm is not None and g == NG - 1:
                        k = tiles.index((t0, w))
                        if k >= 1:
                            norm_mod(next_norm[0], next_norm[1], j_lat, [tiles[k - 1]])
                        if k == len(tiles) - 1:
                            norm_mod(next_norm[0], next_norm[1], j_lat, [tiles[k]])

        def attention(j_lat):
            l = 0
            add("sp", lambda e: e.dma_start(out=cos_t[:, 0:SEQ], in_=cos_d), writes=["cos"], dma=True)
            add("sp", lambda e: e.dma_start(out=sin_t[:, 0:SEQ], in_=sin_d), writes=["sin"], dma=True)
            qk = {"i": 0, "v": 0, "cmb": 0, "st": 0}
            for h in range(NH):
                hsl = h % 2
                wh = wh_s[hsl]
                add("pool", lambda e, wh=wh, h=h: e.dma_start(
                    out=wh.rearrange("p c a n -> p (c a n)"), in_=wattn_d[h].rearrange("p c a n -> p (c a n)")),
                    writes=[("wh", hsl)], dma=True)
                if h % 2 == 0:
                    add("pool", lambda e, h=h: e.dma_start(
                        out=wo_s.rearrange("p a n -> p (a n)"), in_=wo_d[h // 2].rearrange("p a n -> p (a n)")),
                        writes=["wo"], dma=True)
                for (t0, w) in TILES:
                    ti = tix(t0)
                    for kind in (0, 1):
                        if kind == 0 and t0 >= SEQ:
                            continue
                        rope = t0 < SEQ
                        qk["i"] += 1
                        par = qk["i"] % 2
                        pa, pr, pss = (0, 1, 4) if par else (2, 3, 5)
                        wa = 0 if kind == 0 else 2
                        for kc in range(NCH):
                            add("pe", lambda e, kc=kc, wa=wa, pa=pa, t0=t0, w=w, wh=wh: e.matmul(
                                bank(pa, w), wh[:, kc, wa, :], hv(kc, t0, w), start=(kc == 0), stop=(kc == NCH - 1)),
                                reads=[("wh", hsl), ("h", kc, ti)], writes=[("ps", pa)])
                        if rope:
                            for kc in range(NCH):
                                add("pe", lambda e, kc=kc, wa=wa, pr=pr, t0=t0, w=w, wh=wh: e.matmul(
                                    bank(pr, w), wh[:, kc, wa + 1, :], hv(kc, t0, w), start=(kc == 0), stop=(kc == NCH - 1)),
                                    reads=[("wh", hsl), ("h", kc, ti)], writes=[("ps", pr)])
                        add("act", lambda e, pa=pa, w=w: e.activation(sqa[:, 0:w], bank(pa, w), AF.Square),
                            reads=[("ps", pa)], writes=["sqa"])
                        add("pe", lambda e, pss=pss, w=w: e.matmul(bank(pss, w), bones, sqa[:, 0:w], start=True, stop=True),
                            reads=["sqa", "bones"], writes=[("ps", pss)])
                        sdq, rsq, t1, t2 = T_a[0], T_a[1], T_a[2], T_a[3]
                        add("act", lambda e, pss=pss, w=w, sdq=sdq: e.activation(
                            sdq[:, 0:w], bank(pss, w), AF.Sqrt, scale=1.0 / 64, bias=EPS),
                            reads=[("ps", pss)], writes=["Ta0"])
                        add("dve", lambda e, w=w, sdq=sdq, rsq=rsq: e.reciprocal(rsq[:, 0:w], sdq[:, 0:w]),
                            reads=["Ta0"], writes=["Ta1"])
                        gc = gcols[:, 0:1] if kind == 0 else gcols[:, 2:3]
                        gpc = gcols[:, 1:2] if kind == 0 else gcols[:, 3:4]
                        dst = (qT if kind == 0 else kT)[:, t0:t0 + w]
                        dkey = ("qT" if kind == 0 else "kT", ti)
                        fin_scale = 0.125 if kind == 0 else 1.0
                        if rope:
                            add("dve", lambda e, pa=pa, w=w, t1=t1, gc=gc, t0=t0: e.scalar_tensor_tensor(
                                t1[:, 0:w], bank(pa, w), gc, cos_t[:, t0:t0 + w], op0=ALU.mult, op1=ALU.mult),
                                reads=[("ps", pa), "cos", "gcols"], writes=["Ta2"])
                            add("dve", lambda e, pr=pr, w=w, t2=t2, gpc=gpc, t0=t0: e.scalar_tensor_tensor(
                                t2[:, 0:w], bank(pr, w), gpc, sin_t[:, t0:t0 + w], op0=ALU.mult, op1=ALU.mult),
                                reads=[("ps", pr), "sin", "gcols"], writes=["Ta3"])
                            add("pool", lambda e, w=w, t1=t1, t2=t2: e.tensor_tensor(t1[:, 0:w], t1[:, 0:w], t2[:, 0:w], op=ALU.add),
                                reads=["Ta2", "Ta3"], writes=["Ta2"])
                        else:
                            add("dve", lambda e, pa=pa, w=w, t1=t1, gc=gc: e.tensor_scalar(
                                t1[:, 0:w], bank(pa, w), gc, 1.0, op0=ALU.mult, op1=ALU.mult),
                                reads=[("ps", pa), "gcols"], writes=["Ta2"])
                        add("dve", lambda e, w=w, t1=t1, rsq=rsq, dst=dst, fs=fin_scale: e.scalar_tensor_tensor(
                            dst, t1[:, 0:w], fs, rsq[:, 0:w], op0=ALU.mult, op1=ALU.mult),
                            reads=["Ta2", "Ta1"], writes=[dkey])
                for tc0 in range(0, NKC, 4):
                    nch = min(4, NKC - tc0)
                    qk["v"] += 1
                    pv = 6 + qk["v"] % 2
                    for a in range(nch):
                        tok0 = (tc0 + a) * 128
                        for kc in range(NCH):
                            add("pe", lambda e, a=a, kc=kc, tok0=tok0, pv=pv, wh=wh: e.matmul(
                                bank(pv, 128, a * 128), hv(kc, tok0, 128), wh[:, kc, 4, :],
                                start=(kc == 0), stop=(kc == NCH - 1)),
                                reads=[("wh", hsl), ("h", kc, tix(tok0))], writes=[("ps", pv)])
                    copy_alt(v_sb[:, tc0:tc0 + nch, :], bank(pv, nch * 128).rearrange("p (a n) -> p a n", a=nch),
                             reads=[("ps", pv)], writes=[("v", tc0 // 4)])
                npair = NKC // 2
                steps = [(qt, kp) for qt in range(NQT) for kp in range(npair)]
                r, on = T_a[4], T_a[5]
                GQ = min(4, NQT)
                ddg = carve(50 * KB, [128, 1024], F32)
                ssg = carve(54 * KB, [128, 1024], F32)
                KD, KS = ["Ta0", "Ta1"], ["Ta2", "Ta3"]

                def emit_qk(i):
                    qt, kp = steps[i]
                    q0 = qt * 256
                    sset = (qk["st"] + i) % 2
                    bA, bB = sset * 2, sset * 2 + 1
                    for kk in range(2):
                        kc = kp * 2 + kk
                        add("pe", lambda e, kc=kc, kk=kk, bA=bA, q0=q0: e.matmul(
                            bank(bA, 256, kk * 256), kT[0:64, kc * 128:(kc + 1) * 128], qT[0:64, q0:q0 + 256],
                            start=True, stop=True),
                            reads=[("kT", tix(kc * 128)), ("qT", tix(q0))], writes=[("ps", bA)])
                        add("pe", lambda e, kc=kc, kk=kk, bB=bB, q0=q0: e.matmul(
                            bank(bB, 256, kk * 256), kT[64:128, kc * 128:(kc + 1) * 128], qT[64:128, q0:q0 + 256],
                            start=True, stop=True),
                            reads=[("kT", tix(kc * 128)), ("qT", tix(q0))], writes=[("ps", bB)])

                def emit_exp_pv(i):
                    qt, kp = steps[i]
                    sset = (qk["st"] + i) % 2
                    bA, bB = sset * 2, sset * 2 + 1
                    Pb = P_s[sset]
                    ob, sb_ = (4, 5) if qt % 2 == 0 else (6, 7)
                    add("act", lambda e, bA=bA, Pb=Pb: e.activation(
                        Pb.rearrange("p k c q -> p c k q"),
                        psall[:, bA * 512:(bA + 2) * 512].rearrange("p (c k q) -> p c k q", c=2, k=2), AF.Exp),
                        reads=[("ps", bA), ("ps", bB)], writes=[("P", sset)])
                    for kk in range(2):
                        kc = kp * 2 + kk
                        add("pe", lambda e, kc=kc, kk=kk, ob=ob, Pb=Pb: e.matmul(
                            bank(ob), v_sb[:, kc, :], Pb[:, kk].rearrange("p c q -> p (c q)"),
                            start=(kc == 0), stop=(kc == NKC - 1)),
                            reads=[("P", sset), ("v", kc // 4)], writes=[("ps", ob)])
                        add("pe", lambda e, kc=kc, kk=kk, sb_=sb_, Pb=Pb: e.matmul(
                            bank(sb_), ones, Pb[:, kk].rearrange("p c q -> p (c q)"),
                            start=(kc == 0), stop=(kc == NKC - 1)),
                            reads=[("P", sset), "ones"], writes=[("ps", sb_)])

                def combine1(qt):
                    ob, sb_ = (4, 5) if qt % 2 == 0 else (6, 7)
                    g0 = (qt % GQ) * 256
                    add("dve", lambda e, sb_=sb_: e.reciprocal(r, bank(sb_)), reads=[("ps", sb_)], writes=["Ta4"])
                    add("dve", lambda e, ob=ob: e.tensor_tensor(on, bank(ob), r, op=ALU.mult),
                        reads=[("ps", ob), "Ta4"], writes=["Ta5"])
                    add("dve", lambda e, g0=g0: e.scalar_tensor_tensor(
                        ddg[:, g0:g0 + 256], on[:, 256:512], neglam, on[:, 0:256], op0=ALU.mult, op1=ALU.add),
                        reads=["Ta5", "neglam"], writes=KD)

                def combine_sq(qt):
                    g0 = (qt % GQ) * 256
                    add("act", lambda e, g0=g0: e.activation(sqa[:, 0:256], ddg[:, g0:g0 + 256], AF.Square),
                        reads=KD, writes=["sqa"])

                def combine2(qt, h=h):
                    ob, sb_ = (4, 5) if qt % 2 == 0 else (6, 7)
                    g0 = (qt % GQ) * 256
                    add("pe", lambda e, sb_=sb_: e.matmul(bank(sb_, 256), ones, sqa[:, 0:256], start=True, stop=True),
                        reads=["sqa", "ones"], writes=[("ps", sb_)])
                    add("dve", lambda e, sb_=sb_, g0=g0: e.tensor_copy(ssg[:, g0:g0 + 256], bank(sb_, 256)),
                        reads=[("ps", sb_)], writes=KS)
                    if qt % GQ == GQ - 1:
                        W = GQ * 256
                        qb = (qt - (GQ - 1)) * 256
                        add("act", lambda e, W=W: e.activation(ssg[:, 0:W], ssg[:, 0:W], AF.Sqrt, scale=1.0 / 128, bias=EPS),
                            reads=KS, writes=KS)
                        add("dve", lambda e, W=W: e.reciprocal(ssg[:, 0:W], ssg[:, 0:W]), reads=KS, writes=KS)
                        add("dve", lambda e, W=W, qb=qb, h=h: e.scalar_tensor_tensor(
                            onT[:, h % 2, qb:qb + W], ddg[:, 0:W], subg08, ssg[:, 0:W], op0=ALU.mult, op1=ALU.mult),
                            reads=KD + KS + ["subg08"], writes=[("onT", h % 2, qt - k) for k in range(GQ)])

                deferred = []
                emit_qk(0)
                for i in range(len(steps)):
                    qt, kp = steps[i]
                    if i + 1 < len(steps):
                        emit_qk(i + 1)
                    for dfr in list(deferred):
                        if dfr[0] <= i:
                            dfr[1](dfr[2])
                            deferred.remove(dfr)
                    emit_exp_pv(i)
                    if kp == npair - 1:
                        combine1(qt)
                        deferred.append((i + 4, combine_sq, qt))
                        deferred.append((i + 6, combine2, qt))
                for dfr in deferred:
                    dfr[1](dfr[2])
                qk["st"] += len(steps)
                if h % 2 == 1:
                    for (t0, w) in TILES[:NT]:
                        ti = tix(t0)
                        for d in range(NCH):
                            qk["cmb"] += 1
                            ob2 = qk["cmb"] % 4
                            for hh in range(2):
                                add("pe", lambda e, hh=hh, d=d, ob2=ob2, t0=t0, w=w: e.matmul(
                                    bank(ob2, w), wo_s[:, hh, d * 128:(d + 1) * 128], onT[:, hh, t0:t0 + w],
                                    start=(hh == 0), stop=(hh == 1)),
                                    reads=["wo", ("onT", hh, 2 * ti), ("onT", hh, 2 * ti + 1)], writes=[("ps", ob2)])
                            add("dve", lambda e, d=d, ob2=ob2, t0=t0, w=w: e.scalar_tensor_tensor(
                                xv(d, t0, w), bank(ob2, w), Gcol(l, 1, d, j_lat), xv(d, t0, w), op0=ALU.mult, op1=ALU.add),
                                reads=[("ps", ob2), ("x", d, ti), ("Gh", l, 1)], writes=[("x", d, ti)])

        def fourier(j_lat):
            l = 1
            NNT = SEQ // 128
            NKT = SEQ // 256
            add("sp", lambda e: e.dma_start(out=dftc_t.rearrange("p t c n -> p t (c n)"),
                                            in_=dftc_d.rearrange("t p c n -> p t (c n)")), writes=["dftc"], dma=True)
            fc = {"i": 0, "o": 0}
            for hf in range(2):
                add("pool", lambda e, hf=hf: e.dma_start(
                    out=wf_s.rearrange("p a n -> p (a n)"), in_=wf_d[hf].rearrange("p a n -> p (a n)")),
                    writes=["wf"], dma=True)
                for nt in range(NNT):
                    for tb in range(2):
                        pb = tb * 2 + nt % 2
                        for gq in range(2):
                            grp = 2 * hf + gq
                            for cc in range(2):
                                add("pe", lambda e, nt=nt, tb=tb, pb=pb, gq=gq, grp=grp, cc=cc: e.matmul(
                                    bank(pb, 256, gq * 256), hb[:, grp * 2 + cc, nt * 128:(nt + 1) * 128], dftc_t[:, tb, cc, :],
                                    start=(cc == 0), stop=(cc == 1)),
                                    reads=[("h", grp * 2 + cc, tix(nt * 128)), "dftc"], writes=[("ps", pb)])
                        U = U_c if tb == 0 else U_s
                        copy_alt(U[:, nt, :], bank(pb), reads=[("ps", pb)], writes=[("U", tb, nt)])
                for kt in range(NKT):
                    fc["i"] += 1
                    dsl = fc["i"] % 2
                    tabs = dft_s[dsl]
                    add("sp", lambda e, tabs=tabs, kt=kt: e.dma_start(
                        out=tabs.rearrange("p t a n -> p t (a n)"), in_=dftn_d[:, kt].rearrange("t p a n -> p t (a n)")),
                        writes=[("dft", dsl)], dma=True)
                    fT = fT_s[dsl]
                    for kcc in range(4):
                        pb = 4 + kcc % 2
                        for tb in range(2):
                            U = U_c if tb == 0 else U_s
                            for nt in range(NNT):
                                add("pe", lambda e, U=U, tb=tb, nt=nt, kcc=kcc, pb=pb, tabs=tabs: e.matmul(
                                    bank(pb, 256), U[:, nt, kcc * 128:(kcc + 1) * 128], tabs[:, tb, nt, :],
                                    start=(tb == 0 and nt == 0), stop=(tb == 1 and nt == NNT - 1)),
                                    reads=[("U", tb, nt), ("dft", dsl)], writes=[("ps", pb)])
                        copy_alt(fT[:, kcc, :], bank(pb, 256), reads=[("ps", pb)], writes=[("fT", dsl, kcc)])
                    t0 = kt * 256
                    ti = tix(t0)
                    for d in range(NCH):
                        fc["o"] += 1
                        ob = 6 + fc["o"] % 2
                        for kcc in range(4):
                            add("pe", lambda e, kcc=kcc, d=d, ob=ob, fT=fT: e.matmul(
                                bank(ob, 256), wf_s[:, kcc, d * 128:(d + 1) * 128], fT[:, kcc, :],
                                start=(kcc == 0), stop=(kcc == 3)),
                                reads=["wf", ("fT", dsl, kcc)], writes=[("ps", ob)])
                        add("dve", lambda e, d=d, ob=ob, t0=t0: e.scalar_tensor_tensor(
                            xs[:, d, t0:t0 + 256], bank(ob, 256), Gcol(l, 1, d, j_lat), xs[:, d, t0:t0 + 256],
                            op0=ALU.mult, op1=ALU.add),
                            reads=[("ps", ob), ("x", d, ti), ("Gh", l, 1)], writes=[("x", d, ti)])

        LAT = TILES[:NT]
        for bi in range(NB):
            j = bi
            load_stream(bi)
            norm_mod(0, 0, j, TILES)
            ffn(0, 0, j, TILES, next_norm=(0, 1))
            S.barrier()
            attention(j)
            S.barrier()
            norm_mod(0, 2, j, LAT)
            ffn(0, 1, j, LAT, next_norm=(1, 0))
            ffn(1, 0, j, LAT, next_norm=(1, 1))
            S.barrier()
            fourier(j)
            S.barrier()
            norm_mod(1, 2, j, LAT)
            ffn(1, 1, j, LAT)
            store_stream(bi)
        outkeys = [k for k in S.last_writer if isinstance(k, tuple) and k[0] == "out"]
        add("sp", lambda e: e.nop(), reads=outkeys)

        S.finalize()
        with nc.Block() as block:
            @block.sync
            def _(e):
                S.emit(sems, "sp", e)

            @block.tensor
            def _(e):
                S.emit(sems, "pe", e)

            @block.scalar
            def _(e):
                S.emit(sems, "act", e)

            @block.vector
            def _(e):
                S.emit(sems, "dve", e)

            @block.gpsimd
            def _(e):
                S.emit(sems, "pool", e)
    return nc


def make_shared_inputs(SEQ, DFF, inp):
    NG = DFF // 256
    f = np.float32
    sh = {}
    sh["gT"] = np.ascontiguousarray(np.asarray(inp["norm_g"], f).reshape(2, 3, NCH, 128).transpose(3, 0, 1, 2))
    sh["bT"] = np.ascontiguousarray(np.asarray(inp["b_mod"], f).reshape(2, 72, 128).transpose(2, 0, 1))
    wm = np.asarray(inp["w_mod"], f).reshape(2, NCH, 128, 9, 1024)
    sh["wmod"] = np.ascontiguousarray(wm.transpose(0, 3, 2, 1, 4))
    wgu = np.stack([np.asarray(inp["ffn1_w_gu"], f), np.asarray(inp["ffn2_w_gu"], f)], axis=1)
    wgu = wgu.reshape(2, 2, NCH, 128, 2, NG, 256)
    sh["wgu"] = np.ascontiguousarray(wgu.transpose(0, 1, 5, 3, 2, 4, 6)).reshape(2, 2, NG, 128, NCH, 512)
    wd = np.stack([np.asarray(inp["ffn1_w_d"], f), np.asarray(inp["ffn2_w_d"], f)], axis=1)
    wd = wd.reshape(2, 2, NG, 2, 128, 1024)
    sh["wd"] = np.ascontiguousarray(wd.transpose(0, 1, 2, 4, 3, 5))
    wqkv = np.asarray(inp["attn_w_qkv"], f)[0]
    perm = _partner_perm()
    wq = wqkv[:, 0:1024].reshape(NCH, 128, NH, 128)
    wk = wqkv[:, 1024:2048].reshape(NCH, 128, NH, 128)
    wv = wqkv[:, 2048:3072].reshape(NCH, 128, NH, 128)
    wa = np.stack([wq, wq[..., perm], wk, wk[..., perm], wv], axis=3)
    sh["wattn"] = np.ascontiguousarray(wa.transpose(2, 1, 0, 3, 4))
    wo = np.asarray(inp["attn_w_o"], f)[0].reshape(NH // 2, 2, 128, 1024)
    sh["wo"] = np.ascontiguousarray(wo.transpose(0, 2, 1, 3))
    wf = np.asarray(inp["fourier_w"], f)[0].reshape(2, 4, 128, 1024)
    sh["wf"] = np.ascontiguousarray(wf.transpose(0, 2, 1, 3))
    qg = np.asarray(inp["attn_q_g"], f)[0]
    kg = np.asarray(inp["attn_k_g"], f)[0]
    sg = np.asarray(inp["attn_sub_g"], f)[0]
    qg2, kg2 = np.tile(qg, 2), np.tile(kg, 2)
    sh["gcols"] = np.ascontiguousarray(np.stack([qg2, qg2[perm], kg2, kg2[perm], sg], axis=1))
    sh["lamv"] = np.ascontiguousarray(np.stack([np.asarray(inp[k], f)[0] for k in
                                                ("attn_lam_q1", "attn_lam_k1", "attn_lam_q2", "attn_lam_k2")]))
    sh["ident"] = np.eye(128, dtype=f)
    bo = np.zeros((128, 128), f)
    bo[:64, :64] = 1.0
    bo[64:, 64:] = 1.0
    sh["bones"] = bo
    sh["ropecos"], sh["ropesin"] = _rope_tables(SEQ)
    sh["dftn"], sh["dftc"] = _dft_tables(SEQ)
    return sh


def make_in_maps(SEQ, DFF, NB, ncores, inp):
    sh = make_shared_inputs(SEQ, DFF, inp)
    x = np.asarray(inp["x"], np.float32)
    ctx = np.asarray(inp["ctx"], np.float32)
    c = np.asarray(inp["c"], np.float32)
    c_ctx = np.asarray(inp["c_ctx"], np.float32)
    maps = []
    for i in range(ncores):
        b0 = i * NB
        conds = [c[b0 + min(k, NB - 1)] for k in range(2)] + [c_ctx]
        cond = np.stack(conds)
        condT = np.ascontiguousarray(cond.reshape(3, NCH, 128).transpose(2, 1, 0))
        m = dict(sh)
        m["x"] = np.ascontiguousarray(x[b0:b0 + NB])
        m["ctx"] = np.ascontiguousarray(ctx[b0:b0 + NB])
        m["condT"] = condT
        maps.append(m)
    return maps


_CACHE = {}


def kernel(x, c, ctx, c_ctx, norm_g, w_mod, b_mod, ffn1_w_gu, ffn1_w_d, ffn2_w_gu, ffn2_w_d,
           attn_w_qkv, attn_w_o, attn_q_g, attn_k_g, attn_lam_q1, attn_lam_k1, attn_lam_q2,
           attn_lam_k2, attn_sub_g, fourier_w):
    inp = dict(x=x, c=c, ctx=ctx, c_ctx=c_ctx, norm_g=norm_g, w_mod=w_mod, b_mod=b_mod,
               ffn1_w_gu=ffn1_w_gu, ffn1_w_d=ffn1_w_d, ffn2_w_gu=ffn2_w_gu, ffn2_w_d=ffn2_w_d,
               attn_w_qkv=attn_w_qkv, attn_w_o=attn_w_o, attn_q_g=attn_q_g, attn_k_g=attn_k_g,
               attn_lam_q1=attn_lam_q1, attn_lam_k1=attn_lam_k1, attn_lam_q2=attn_lam_q2,
               attn_lam_k2=attn_lam_k2, attn_sub_g=attn_sub_g, fourier_w=fourier_w)
    B, SEQ, _ = np.shape(x)
    DFF = np.shape(ffn1_w_d)[1]
    ncores = 8
    NB = B // ncores
    nc = build_program(SEQ, DFF, NB)
    maps = make_in_maps(SEQ, DFF, NB, ncores, inp)
    res = run_bass_kernel_spmd(nc, maps, core_ids=list(range(ncores)))
    out = np.concatenate([np.asarray(r["out"], np.float32) for r in res.results], axis=0)
    return out
```

```python
import math
from contextlib import ExitStack
import numpy as np
import ml_dtypes
import concourse.bass as bass
import concourse.mybir as mybir
from concourse.bass_utils import run_bass_kernel_spmd

F32 = mybir.dt.float32
BF16 = mybir.dt.bfloat16
AF = mybir.ActivationFunctionType
ALU = mybir.AluOpType
AX = mybir.AxisListType

D = 1024
NCH = 8
NH = 8
CTX = 256
EPS = 1e-6
NSLOT = 8
STRICT_SAME_ENGINE = True
LAMBDA_INIT0 = 0.8 - 0.6 * math.exp(-0.3 * 0)


class Sched:
    def __init__(self):
        self.threads = ["pe", "act", "dve", "pool", "sp"]
        self.dma_queues = ["sp", "pool"]
        self.counters = list(self.threads)
        for q in self.dma_queues:
            for s in range(NSLOT):
                self.counters.append(("dma", q, s))
        self.cidx = {c: i for i, c in enumerate(self.counters)}
        self.ops = []
        self.last_writer = {}
        self.readers = {}
        self.dma_count = {q: 0 for q in self.dma_queues}
        self.slot_last = {}
        self.last_on = {}
        self.barrier_deps = {}

    def barrier(self):
        deps = set(self.last_on.values())
        for t in self.threads:
            self.barrier_deps.setdefault(t, set()).update(deps)

    def add(self, thread, fn, reads=(), writes=(), dma=False):
        i = len(self.ops)
        raw, other = set(), set()
        for r in reads:
            w = self.last_writer.get(r)
            if w is not None:
                raw.add(w)
        for k in writes:
            w = self.last_writer.get(k)
            if w is not None:
                other.add(w)
            rd = self.readers.get(k)
            if rd:
                other.update(rd.values())
        bd = self.barrier_deps.pop(thread, None)
        if bd:
            raw.update(bd)
        if dma:
            n = self.dma_count[thread]
            self.dma_count[thread] = n + 1
            counter = self.cidx[("dma", thread, n % NSLOT)]
            prev = self.slot_last.get(counter)
            if prev is not None:
                other.add(prev)
            self.slot_last[counter] = i
        else:
            counter = self.cidx[thread]
        self.last_on[counter] = i
        for r in reads:
            self.readers.setdefault(r, {})[counter] = i
        for k in writes:
            self.last_writer[k] = i
            self.readers[k] = {}
        raw.discard(i)
        other.discard(i)
        other -= raw
        self.ops.append(dict(thread=thread, fn=fn, dma=dma, counter=counter,
                             raw=sorted(raw), other=sorted(other)))
        return i

    def finalize(self):
        nC = len(self.counters)
        K = {t: [-1] * nC for t in self.threads}
        pos = {t: 0 for t in self.threads}
        ops = self.ops
        VC = [None] * len(ops)
        signaled = [False] * len(ops)
        for i, op in enumerate(ops):
            t = op["thread"]
            k = K[t]
            waits = []
            op["pos"] = pos[t]
            pos[t] += 1
            for kind, deps in (("raw", op["raw"]), ("other", op["other"])):
                for j in deps:
                    oj = ops[j]
                    if k[oj["counter"]] >= j:
                        continue
                    if (not oj["dma"]) and (not op["dma"]) and oj["thread"] == t:
                        if t == "pe" or kind == "other":
                            continue
                        if (not STRICT_SAME_ENGINE) and op["pos"] - oj["pos"] >= 3:
                            continue
                    waits.append(j)
                    signaled[j] = True
                    vj = VC[j]
                    for c in range(nC):
                        if vj[c] > k[c]:
                            k[c] = vj[c]
            op["waits"] = waits
            v = list(k)
            v[op["counter"]] = i
            VC[i] = v
            if op["dma"]:
                signaled[i] = True
        cnt = [0] * nC
        for i, op in enumerate(ops):
            if signaled[i]:
                c = op["counter"]
                cnt[c] += 16 if op["dma"] else 1
                op["sigval"] = cnt[c]
            else:
                op["sigval"] = None

    def emit(self, sems, thread, eng):
        ops = self.ops
        for op in ops:
            if op["thread"] != thread:
                continue
            for j in op["waits"]:
                oj = ops[j]
                eng.wait_ge(sems[oj["counter"]], oj["sigval"])
            ins = op["fn"](eng)
            if op["sigval"] is not None:
                ins.then_inc(sems[op["counter"]], 16 if op["dma"] else 1)


def _bf(a):
    return np.ascontiguousarray(a).astype(ml_dtypes.bfloat16)


def _partner_perm():
    p = np.arange(128)
    half = (p // 16) % 2
    return np.where(half == 0, p + 16, p - 16)


def _rope_tables(seq):
    grid_w = 64
    pairs = 16
    t = np.arange(seq)
    row = (t // grid_w).astype(np.float32)
    col = (t % grid_w).astype(np.float32)
    inv_freq = (10000.0 ** (-(np.arange(pairs, dtype=np.float32) / pairs))).astype(np.float32)
    p = np.arange(128)
    d = p % 64
    axis = d // 32
    half = (d // 16) % 2
    pair = d % 16
    pos = np.where(axis[:, None] == 0, row[None, :], col[None, :]).astype(np.float32)
    ang = (pos * inv_freq[pair][:, None]).astype(np.float32)
    cos = np.cos(ang).astype(np.float32)
    sin = np.sin(ang).astype(np.float32)
    sgn = np.where(half == 0, -1.0, 1.0).astype(np.float32)[:, None]
    return np.ascontiguousarray(cos), np.ascontiguousarray(sin * sgn)


def _dft_tables(seq):
    C = 256
    n = np.arange(seq, dtype=np.int64)
    m = (n[:, None] * n[None, :]) % seq
    ang = 2.0 * np.pi * m.astype(np.float64) / seq
    scale = 1.0 / math.sqrt(seq * C)
    cn = (np.cos(ang) * scale)
    sn = (-np.sin(ang) * scale)
    nkt = seq // 256
    nnt = seq // 128
    tab = np.stack([cn, sn]).reshape(2, nnt, 128, nkt, 256).transpose(0, 3, 2, 1, 4)
    c = np.arange(C, dtype=np.int64)
    mc = (c[:, None] * c[None, :]) % C
    angc = 2.0 * np.pi * mc.astype(np.float64) / C
    tc = np.stack([np.cos(angc), np.sin(angc)]).reshape(2, 2, 128, 256).transpose(0, 2, 1, 3)
    return _bf(tab), _bf(tc)


def build_program(SEQ, DFF, NB):
    NT = SEQ // 512
    TILES = [(i * 512, 512) for i in range(NT)] + [(SEQ, CTX)]
    NTOK = SEQ + CTX
    NKC = NTOK // 128
    NFC = DFF // 128
    NG = NFC // 2
    NQT = SEQ // 256

    nc = bass.Bass("TRN2", target_bir_lowering=False)
    dr = {}

    def din(name, shape, dt=F32):
        dr[name] = nc.dram_tensor(name, list(shape), dt, kind="ExternalInput").ap()
        return dr[name]

    x_d = din("x", [NB, SEQ, D])
    ctx_d = din("ctx", [NB, CTX, D])
    condT_d = din("condT", [128, NCH, 3])
    gT_d = din("gT", [128, 2, 3, NCH])
    bT_d = din("bT", [128, 2, 72])
    wmod_d = din("wmod", [2, 9, 128, NCH, 1024])
    wgu_d = din("wgu", [2, 2, NG, 128, NCH, 512])
    wd_d = din("wd", [2, 2, NG, 128, 2, 1024])
    wattn_d = din("wattn", [NH, 128, NCH, 5, 128])
    wo_d = din("wo", [NH // 2, 128, 2, 1024])
    wf_d = din("wf", [2, 128, 4, 1024])
    gcols_d = din("gcols", [128, 5])
    lamv_d = din("lamv", [4, 64])
    ident_d = din("ident", [128, 128])
    bones_d = din("bones", [128, 128])
    cos_d = din("ropecos", [128, SEQ])
    sin_d = din("ropesin", [128, SEQ])
    dftn_d = din("dftn", [2, SEQ // 256, 128, SEQ // 128, 256], BF16)
    dftc_d = din("dftc", [2, 128, 2, 256], BF16)
    out_d = nc.dram_tensor("out", [NB, SEQ, D], F32, kind="ExternalOutput").ap()

    S = Sched()
    add = S.add
    es = ExitStack()
    with es:
        xs = es.enter_context(nc.sbuf_tensor("xs", [128, NCH, SEQ], F32))
        hb = es.enter_context(nc.sbuf_tensor("hb", [128, NCH, SEQ], BF16))
        SCR_K = 82
        scr = es.enter_context(nc.sbuf_tensor("scr", [128, SCR_K * 512], BF16))
        pers = es.enter_context(nc.sbuf_tensor("pers", [128, 1600], F32))
        persb = es.enter_context(nc.sbuf_tensor("persb", [128, 512], BF16))
        psall = es.enter_context(nc.psum_tensor("psall", [128, 4096], F32))
        sems = [es.enter_context(nc.semaphore(f"s{i}")) for i in range(len(S.counters))]

        def bank(i, w=512, off=0):
            return psall[:, i * 512 + off: i * 512 + off + w]

        def carve(off_b, shape, dt):
            esz = 4 if dt == F32 else 2
            n = int(np.prod(shape[1:]))
            assert off_b % 4 == 0
            assert off_b + n * esz <= SCR_K * 1024, (off_b, shape)
            a = scr[:, off_b // 2: off_b // 2 + n * esz // 2]
            if dt == F32:
                a = a.bitcast(F32)
            if len(shape) == 3:
                a = a.rearrange("p (a b) -> p a b", b=shape[2])
            elif len(shape) == 4:
                a = a.rearrange("p (a b c) -> p a b c", b=shape[2], c=shape[3])
            return a

        KB = 1024
        po = [0]

        def ptake(n):
            a = pers[:, po[0]: po[0] + n]
            po[0] += n
            assert po[0] <= 1600
            return a
        ident = ptake(128)
        modsT = ptake(2 * 72 * 3).rearrange("p (l n j) -> p l n j", l=2, n=72)
        Amod = ptake(2 * 3 * 8 * 3).rearrange("p (l s c j) -> p l s c j", l=2, s=3, c=8)
        Gh = ptake(2 * 3 * 8 * 3).rearrange("p (l s c j) -> p l s c j", l=2, s=3, c=8)
        gT = ptake(2 * 3 * 8).rearrange("p (l s c) -> p l s c", l=2, s=3)
        bT = ptake(2 * 72).rearrange("p (l n) -> p l n", l=2)
        condT = ptake(24).rearrange("p (c j) -> p c j", j=3)
        gcols = ptake(5)
        lamv = ptake(256).rearrange("p (a b) -> p a b", a=4)
        lamt = ptake(128).rearrange("p (a b) -> p a b", a=2)
        lams = ptake(8)
        ones = persb[:, 0:128]
        bones = persb[:, 128:256]
        scT = persb[:, 256:280].rearrange("p (c j) -> p c j", j=3)

        wgu_s = [carve(i * 8 * KB, [128, NCH, 512], BF16) for i in range(2)]
        wd_s = [carve(16 * KB + i * 4 * KB, [128, 2, 1024], BF16) for i in range(2)]
        aT_s = [carve(24 * KB + i * 2 * KB, [128, 2, 512], BF16) for i in range(2)]
        sg_s = [carve(28 * KB + i * 2 * KB, [128, 512], F32) for i in range(2)]
        sqb_s = [carve(32 * KB + i * KB, [128, 512], BF16) for i in range(2)]
        sd_f = carve(34 * KB, [128, 512], F32)
        rstd_f = carve(36 * KB, [128, 512], F32)
        tmp_s = [carve(38 * KB + i * 2 * KB, [128, 512], F32) for i in range(2)]
        stage = [carve(42 * KB + i * 4 * KB, [128, 1024], F32) for i in range(2)]
        ys = carve(50 * KB, [128, NCH, CTX], F32)
        hy = carve(58 * KB, [128, NCH, CTX], BF16)
        wm_s = [carve(i * 16 * KB, [128, NCH, 1024], BF16) for i in range(2)]
        wh_s = [carve(i * 10 * KB, [128, NCH, 5, 128], BF16) for i in range(2)]
        wo_s = carve(20 * KB, [128, 2, 1024], BF16)
        qT = carve(24 * KB, [128, 2048], BF16)
        kT = carve(28 * KB, [128, 2304], BF16)
        v_sb = carve(28 * KB + 4608, [128, 18, 128], BF16)
        onT = carve(37 * KB, [128, 2, 2048], BF16)
        P_s = [carve(45 * KB + i * 2 * KB, [128, 2, 2, 256], BF16) for i in range(2)]
        sqa = carve(49 * KB, [128, 512], BF16)
        T_a = [carve(50 * KB + i * 2 * KB, [128, 512], F32) for i in range(4)]
        cos_t = carve(62 * KB, [128, 2048], F32)
        sin_t = carve(70 * KB, [128, 2048], F32)
        T_a += [carve(78 * KB + i * 2 * KB, [128, 512], F32) for i in range(2)]
        NNT_ = SEQ // 128
        U_c = carve(0, [128, NNT_, 512], BF16)
        U_s = carve(16 * KB, [128, NNT_, 512], BF16)
        dft_s = [carve(32 * KB + i * 16 * KB, [128, 2, NNT_, 256], BF16) for i in range(2)]
        wf_s = carve(64 * KB, [128, 4, 1024], BF16)
        fT_s = [carve(72 * KB + i * 2 * KB, [128, 4, 256], BF16) for i in range(2)]
        dftc_t = carve(76 * KB, [128, 2, 2, 256], BF16)

        def xv(c, t0, w):
            if t0 < SEQ:
                return xs[:, c, t0:t0 + w]
            return ys[:, c, t0 - SEQ:t0 - SEQ + w]

        def hv(c, t0, w):
            if t0 < SEQ:
                return hb[:, c, t0:t0 + w]
            return hy[:, c, t0 - SEQ:t0 - SEQ + w]

        def tix(t0):
            return t0 // 512

        cnt = {"cp": 0}

        def copy_alt(out, in_, reads, writes):
            cnt["cp"] += 1
            if cnt["cp"] % 2:
                add("act", lambda e: e.activation(out, in_, AF.Copy), reads=reads, writes=writes)
            else:
                add("dve", lambda e: e.tensor_copy(out, in_), reads=reads, writes=writes)

        def load_const(dst, src, key):
            add("sp", lambda e: e.dma_start(out=dst, in_=src), writes=[key], dma=True)
        load_const(ident, ident_d, "ident")
        load_const(condT.rearrange("p c j -> p (c j)"), condT_d.rearrange("p c j -> p (c j)"), "condT")
        load_const(gT.rearrange("p l s c -> p (l s c)"), gT_d.rearrange("p l s c -> p (l s c)"), "gT")
        load_const(bT.rearrange("p l n -> p (l n)"), bT_d.rearrange("p l n -> p (l n)"), "bT")
        load_const(gcols, gcols_d, "gcols")
        for a in range(4):
            load_const(lamv[:, a, :], lamv_d[a].partition_broadcast(128), "lamv")
        add("pool", lambda e: e.memset(ones, 1.0), writes=["ones"])
        add("pool", lambda e: e.dma_start(out=bones, in_=bones_d), writes=["bones"], dma=True)

        add("dve", lambda e: e.tensor_tensor(lamt[:, 0, :], lamv[:, 0, :], lamv[:, 1, :], op=ALU.mult), reads=["lamv"], writes=["lamt0"])
        add("dve", lambda e: e.tensor_tensor(lamt[:, 1, :], lamv[:, 2, :], lamv[:, 3, :], op=ALU.mult), reads=["lamv"], writes=["lamt1"])
        add("dve", lambda e: e.tensor_reduce(lams[:, 0:1], lamt[:, 0, :], axis=AX.X, op=ALU.add), reads=["lamt0"], writes=["lams0"])
        add("dve", lambda e: e.tensor_reduce(lams[:, 1:2], lamt[:, 1, :], axis=AX.X, op=ALU.add), reads=["lamt1"], writes=["lams1"])
        add("act", lambda e: e.activation(lams[:, 2:4], lams[:, 0:2], AF.Exp), reads=["lams0", "lams1"], writes=["lams2"])
        add("dve", lambda e: e.tensor_tensor(lams[:, 4:5], lams[:, 3:4], lams[:, 2:3], op=ALU.subtract), reads=["lams2"], writes=["lams4a"])
        add("dve", lambda e: e.tensor_scalar(lams[:, 5:6], lams[:, 4:5], -LAMBDA_INIT0, 1.0, op0=ALU.add, op1=ALU.mult), reads=["lams4a"], writes=["neglam"])
        neglam = lams[:, 5:6]
        add("dve", lambda e: e.tensor_scalar(lams[:, 6:7], gcols[:, 4:5], 1.0 - LAMBDA_INIT0, 0.0, op0=ALU.mult, op1=ALU.add), reads=["gcols"], writes=["subg08"])
        subg08 = lams[:, 6:7]

        add("act", lambda e: e.activation(scT.rearrange("p c j -> p (c j)"), condT.rearrange("p c j -> p (c j)"), AF.Silu),
            reads=["condT"], writes=["scT"])
        wm1 = carve(62 * KB, [128, NCH, 1024], BF16)

        def mods_slab_dma(l, m):
            add("pool", lambda e, l=l, m=m: e.dma_start(
                out=wm1.rearrange("p c n -> p (c n)"), in_=wmod_d[l, m].rearrange("p c n -> p (c n)")),
                writes=["wm1"], dma=True)

        def mods_slab_mm(l, m):
            pb = bank(4)
            for nn in range(8):
                for kc in range(NCH):
                    add("pe", lambda e, pb=pb, nn=nn, kc=kc: e.matmul(
                        pb[:, nn * 3:nn * 3 + 3], wm1[:, kc, nn * 128:(nn + 1) * 128], scT[:, kc, :],
                        start=(kc == 0), stop=(kc == NCH - 1)),
                        reads=["wm1", "scT"], writes=[("ps", 4)])
            for j in range(3):
                add("dve", lambda e, pb=pb, l=l, m=m, j=j: e.tensor_tensor(
                    modsT[:, l, m * 8:(m + 1) * 8, j], pb[:, 0:24].rearrange("p (n j) -> p n j", j=3)[:, :, j],
                    bT[:, l, m * 8:(m + 1) * 8], op=ALU.add),
                    reads=[("ps", 4), "bT"], writes=[("mods", l, m)])

        def mods_derive(l, s):
            for j in range(3):
                add("dve", lambda e, l=l, s=s, j=j: e.scalar_tensor_tensor(
                    Amod[:, l, s, :, j], modsT[:, l, (3 * s + 1) * 8:(3 * s + 2) * 8, j], 1.0, gT[:, l, s, :],
                    op0=ALU.add, op1=ALU.mult),
                    reads=[("mods", l, 3 * s + 1), "gT"], writes=[("Amod", l, s)])
                add("dve", lambda e, l=l, s=s, j=j: e.tensor_scalar(
                    Gh[:, l, s, :, j], modsT[:, l, (3 * s + 2) * 8:(3 * s + 3) * 8, j],
                    (1.0 if s == 1 else 0.5), 0.0, op0=ALU.mult, op1=ALU.add),
                    reads=[("mods", l, 3 * s + 2)], writes=[("Gh", l, s)])

        for m in range(3):
            mods_slab_dma(0, m)
            mods_slab_mm(0, m)
        mods_derive(0, 0)
        EXTRA = {
            (0, 0): [("slab", 0, 3), ("slab", 0, 4), ("slab", 0, 5), ("derive", 0, 1),
                     ("slab", 0, 6), ("slab", 0, 7), ("slab", 0, 8), ("derive", 0, 2)],
            (0, 1): [("slab", 1, 0), ("slab", 1, 1), ("slab", 1, 2), ("derive", 1, 0),
                     ("slab", 1, 3), ("slab", 1, 4), ("slab", 1, 5), ("derive", 1, 1),
                     ("slab", 1, 6), ("slab", 1, 7), ("slab", 1, 8), ("derive", 1, 2)],
        }

        def Acol(l, s, c, j):
            return Amod[:, l, s, c, j:j + 1]

        def Bcol(l, s, c, j):
            return modsT[:, l, (3 * s) * 8 + c, j:j + 1]

        def Gcol(l, s, c, j):
            return Gh[:, l, s, c, j:j + 1]

        S.barrier()

        def load_stream(bi):
            chunks = [(x_d[bi, t0:t0 + 128, :], t0) for t0 in range(0, SEQ, 128)]
            chunks += [(ctx_d[bi, t0:t0 + 128, :], SEQ + t0) for t0 in range(0, CTX, 128)]
            for ci, (src, t0) in enumerate(chunks):
                st = stage[ci % 2]
                add("sp", lambda e, st=st, src=src: e.dma_start(out=st, in_=src), writes=[("stage", ci % 2)], dma=True)
                for half in range(2):
                    b = (ci % 2) * 2 + half
                    for cc in range(4):
                        c = half * 4 + cc
                        add("pe", lambda e, b=b, cc=cc, c=c, st=st: e.transpose(
                            bank(b, 128, cc * 128), st[:, c * 128:(c + 1) * 128], ident),
                            reads=[("stage", ci % 2), "ident"], writes=[("ps", b)])
                    if t0 < SEQ:
                        dst = xs[:, half * 4:half * 4 + 4, t0:t0 + 128]
                    else:
                        dst = ys[:, half * 4:half * 4 + 4, t0 - SEQ:t0 - SEQ + 128]
                    copy_alt(dst, bank(b).rearrange("p (c t) -> p c t", c=4), reads=[("ps", b)],
                             writes=[("x", half * 4 + cc, tix(t0)) for cc in range(4)])

        def store_stream(bi):
            for ci, t0 in enumerate(range(0, SEQ, 128)):
                st = stage[ci % 2]
                for half in range(2):
                    b = (ci % 2) * 2 + half
                    for cc in range(4):
                        c = half * 4 + cc
                        add("pe", lambda e, b=b, cc=cc, c=c, t0=t0: e.transpose(
                            bank(b, 128, cc * 128), xs[:, c, t0:t0 + 128], ident),
                            reads=[("x", c, tix(t0)), "ident"], writes=[("ps", b)])
                    copy_alt(st[:, half * 512:(half + 1) * 512], bank(b), reads=[("ps", b)],
                             writes=[("stage", ci % 2, half)])
                add("sp", lambda e, st=st, t0=t0: e.dma_start(out=out_d[bi, t0:t0 + 128, :], in_=st),
                    reads=[("stage", ci % 2, 0), ("stage", ci % 2, 1)], writes=[("stage", ci % 2), ("out", bi, ci)], dma=True)

        nrm = {"i": 0}

        def norm_mod(l, s, j_lat, tiles):
            for (t0, w) in tiles:
                j = j_lat if t0 < SEQ else 2
                ti = tix(t0)
                nrm["i"] += 1
                pb = 6 + nrm["i"] % 2
                for c in range(NCH):
                    sq = sqb_s[c % 2]
                    add("act", lambda e, sq=sq, c=c, t0=t0, w=w: e.activation(sq[:, 0:w], xv(c, t0, w), AF.Square),
                        reads=[("x", c, ti)], writes=[("sqb", c % 2)])
                    add("pe", lambda e, sq=sq, c=c, w=w, pb=pb: e.matmul(
                        bank(pb, w), ones, sq[:, 0:w], start=(c == 0), stop=(c == NCH - 1)),
                        reads=[("sqb", c % 2), "ones"], writes=[("ps", pb)])
                add("act", lambda e, w=w, pb=pb: e.activation(sd_f[:, 0:w], bank(pb, w), AF.Sqrt, scale=1.0 / D, bias=EPS),
                    reads=[("ps", pb)], writes=["sd"])
                add("dve", lambda e, w=w: e.reciprocal(rstd_f[:, 0:w], sd_f[:, 0:w]), reads=["sd"], writes=["rstd"])
                for c in range(NCH):
                    tm = tmp_s[c % 2]
                    add("dve", lambda e, tm=tm, c=c, t0=t0, w=w: e.tensor_tensor(
                        tm[:, 0:w], xv(c, t0, w), rstd_f[:, 0:w], op=ALU.mult),
                        reads=[("x", c, ti), "rstd"], writes=[("tmp", c % 2)])
                    add("act", lambda e, tm=tm, c=c, t0=t0, w=w, j=j: e.activation(
                        hv(c, t0, w), tm[:, 0:w], AF.Identity, bias=Bcol(l, s, c, j), scale=Acol(l, s, c, j)),
                        reads=[("tmp", c % 2), ("Amod", l, s), ("mods", l, 3 * s)], writes=[("h", c, ti)])

        ffc = {"g": 0, "o": 0, "a": 0}

        def ffn(l, which, j_lat, tiles, next_norm=None, extra=()):
            s = 0 if which == 0 else 2
            for g in range(NG):
                gi = ffc["g"]
                ffc["g"] += 1
                sl = gi % 2
                wgu, wd = wgu_s[sl], wd_s[sl]
                add("pool", lambda e, wgu=wgu, g=g: e.dma_start(
                    out=wgu.rearrange("p c n -> p (c n)"), in_=wgu_d[l, which, g].rearrange("p c n -> p (c n)")),
                    writes=[("wgu", sl)], dma=True)
                add("pool", lambda e, wd=wd, g=g: e.dma_start(
                    out=wd.rearrange("p c n -> p (c n)"), in_=wd_d[l, which, g].rearrange("p c n -> p (c n)")),
                    writes=[("wd", sl)], dma=True)
                items = []
                if extra:
                    extra = list(extra)
                    n_left_groups = NG - g
                    nslab = sum(1 for it in extra if it[0] == "slab")
                    want = -(-nslab // n_left_groups) if nslab else 0
                    while extra and (want > 0 or extra[0][0] == "derive" or n_left_groups == 1):
                        it = extra.pop(0)
                        items.append(it)
                        if it[0] == "slab":
                            want -= 1
                            if want == 0 and n_left_groups > 1 and not (extra and extra[0][0] == "derive"):
                                break

                def run_items():
                    for it in items:
                        if it[0] == "slab":
                            mods_slab_dma(it[1], it[2])
                            mods_slab_mm(it[1], it[2])
                        else:
                            mods_derive(it[1], it[2])
                nslab_here = sum(1 for it in items if it[0] == "slab")
                if g == NG - 1 or nslab_here != 1:
                    run_items()
                    items = []
                else:
                    for it in items:
                        if it[0] == "slab":
                            mods_slab_dma(it[1], it[2])
                for (t0, w) in tiles:
                    j = j_lat if t0 < SEQ else 2
                    ti = tix(t0)
                    ffc["a"] += 1
                    asl = ffc["a"] % 2
                    aT = aT_s[asl]
                    for fj in range(2):
                        gb, ub = fj, 2 + fj
                        for kc in range(NCH):
                            add("pe", lambda e, kc=kc, fj=fj, gb=gb, t0=t0, w=w, wgu=wgu: e.matmul(
                                bank(gb, w), wgu[:, kc, fj * 128:(fj + 1) * 128], hv(kc, t0, w),
                                start=(kc == 0), stop=(kc == NCH - 1)),
                                reads=[("wgu", sl), ("h", kc, ti)], writes=[("ps", gb)])
                        for kc in range(NCH):
                            add("pe", lambda e, kc=kc, fj=fj, ub=ub, t0=t0, w=w, wgu=wgu: e.matmul(
                                bank(ub, w), wgu[:, kc, 256 + fj * 128:256 + (fj + 1) * 128], hv(kc, t0, w),
                                start=(kc == 0), stop=(kc == NCH - 1)),
                                reads=[("wgu", sl), ("h", kc, ti)], writes=[("ps", ub)])
                        sg = sg_s[fj]
                        add("act", lambda e, sg=sg, gb=gb, w=w: e.activation(sg[:, 0:w], bank(gb, w), AF.Silu),
                            reads=[("ps", gb)], writes=[("sg", fj)])
                        add("dve", lambda e, sg=sg, ub=ub, w=w, aT=aT, fj=fj: e.tensor_tensor(
                            aT[:, fj, 0:w], sg[:, 0:w], bank(ub, w), op=ALU.mult),
                            reads=[("sg", fj), ("ps", ub)], writes=[("aT", asl, fj)])
                    for d in range(NCH):
                        ob = 4 + ffc["o"] % 4
                        ffc["o"] += 1
                        for fj in range(2):
                            add("pe", lambda e, fj=fj, d=d, ob=ob, w=w, aT=aT, wd=wd: e.matmul(
                                bank(ob, w), wd[:, fj, d * 128:(d + 1) * 128], aT[:, fj, 0:w],
                                start=(fj == 0), stop=(fj == 1)),
                                reads=[("wd", sl), ("aT", asl, fj)], writes=[("ps", ob)])
                        add("dve", lambda e, d=d, ob=ob, t0=t0, w=w, j=j: e.scalar_tensor_tensor(
                            xv(d, t0, w), bank(ob, w), Gcol(l, s, d, j), xv(d, t0, w), op0=ALU.mult, op1=ALU.add),
                            reads=[("ps", ob), ("x", d, ti), ("Gh", l, s)], writes=[("x", d, ti)])
                    if items and (t0, w) == tiles[-1]:
                        for it in items:
                            if it[0] == "slab":
                                mods_slab_mm(it[1], it[2])
                            else:
                                mods_derive(it[1], it[2])
                    if next_norm is not None and g == NG - 1:
                        k = tiles.index((t0, w))
                        if k >= 1:
                            norm_mod(next_norm[0], next_norm[1], j_lat, [tiles[k - 1]])
                        if k == len(tiles) - 1:
                            norm_mod(next_norm[0], next_norm[1], j_lat, [tiles[k]])

        def attention(j_lat):
            l = 0
            add("sp", lambda e: e.dma_start(out=cos_t[:, 0:SEQ], in_=cos_d), writes=["cos"], dma=True)
            add("sp", lambda e: e.dma_start(out=sin_t[:, 0:SEQ], in_=sin_d), writes=["sin"], dma=True)
            qk = {"i": 0, "v": 0, "cmb": 0, "st": 0}
            for h in range(NH):
                hsl = h % 2
                wh = wh_s[hsl]
                add("pool", lambda e, wh=wh, h=h: e.dma_start(
                    out=wh.rearrange("p c a n -> p (c a n)"), in_=wattn_d[h].rearrange("p c a n -> p (c a n)")),
                    writes=[("wh", hsl)], dma=True)
                if h % 2 == 0:
                    add("pool", lambda e, h=h: e.dma_start(
                        out=wo_s.rearrange("p a n -> p (a n)"), in_=wo_d[h // 2].rearrange("p a n -> p (a n)")),
                        writes=["wo"], dma=True)
                for (t0, w) in TILES:
                    ti = tix(t0)
                    for kind in (0, 1):
                        if kind == 0 and t0 >= SEQ:
                            continue
                        rope = t0 < SEQ
                        qk["i"] += 1
                        par = qk["i"] % 2
                        pa, pr, pss = (0, 1, 4) if par else (2, 3, 5)
                        wa = 0 if kind == 0 else 2
                        for kc in range(NCH):
                            add("pe", lambda e, kc=kc, wa=wa, pa=pa, t0=t0, w=w, wh=wh: e.matmul(
                                bank(pa, w), wh[:, kc, wa, :], hv(kc, t0, w), start=(kc == 0), stop=(kc == NCH - 1)),
                                reads=[("wh", hsl), ("h", kc, ti)], writes=[("ps", pa)])
                        if rope:
                            for kc in range(NCH):
                                add("pe", lambda e, kc=kc, wa=wa, pr=pr, t0=t0, w=w, wh=wh: e.matmul(
                                    bank(pr, w), wh[:, kc, wa + 1, :], hv(kc, t0, w), start=(kc == 0), stop=(kc == NCH - 1)),
                                    reads=[("wh", hsl), ("h", kc, ti)], writes=[("ps", pr)])
                        add("act", lambda e, pa=pa, w=w: e.activation(sqa[:, 0:w], bank(pa, w), AF.Square),
                            reads=[("ps", pa)], writes=["sqa"])
                        add("pe", lambda e, pss=pss, w=w: e.matmul(bank(pss, w), bones, sqa[:, 0:w], start=True, stop=True),
                            reads=["sqa", "bones"], writes=[("ps", pss)])
                        sdq, rsq, t1, t2 = T_a[0], T_a[1], T_a[2], T_a[3]
                        add("act", lambda e, pss=pss, w=w, sdq=sdq: e.activation(
                            sdq[:, 0:w], bank(pss, w), AF.Sqrt, scale=1.0 / 64, bias=EPS),
                            reads=[("ps", pss)], writes=["Ta0"])
                        add("dve", lambda e, w=w, sdq=sdq, rsq=rsq: e.reciprocal(rsq[:, 0:w], sdq[:, 0:w]),
                            reads=["Ta0"], writes=["Ta1"])
                        gc = gcols[:, 0:1] if kind == 0 else gcols[:, 2:3]
                        gpc = gcols[:, 1:2] if kind == 0 else gcols[:, 3:4]
                        dst = (qT if kind == 0 else kT)[:, t0:t0 + w]
                        dkey = ("qT" if kind == 0 else "kT", ti)
                        fin_scale = 0.125 if kind == 0 else 1.0
                        if rope:
                            add("dve", lambda e, pa=pa, w=w, t1=t1, gc=gc, t0=t0: e.scalar_tensor_tensor(
                                t1[:, 0:w], bank(pa, w), gc, cos_t[:, t0:t0 + w], op0=ALU.mult, op1=ALU.mult),
                                reads=[("ps", pa), "cos", "gcols"], writes=["Ta2"])
                            add("dve", lambda e, pr=pr, w=w, t2=t2, gpc=gpc, t0=t0: e.scalar_tensor_tensor(
                                t2[:, 0:w], bank(pr, w), gpc, sin_t[:, t0:t0 + w], op0=ALU.mult, op1=ALU.mult),
                                reads=[("ps", pr), "sin", "gcols"], writes=["Ta3"])
                            add("pool", lambda e, w=w, t1=t1, t2=t2: e.tensor_tensor(t1[:, 0:w], t1[:, 0:w], t2[:, 0:w], op=ALU.add),
                                reads=["Ta2", "Ta3"], writes=["Ta2"])
                        else:
                            add("dve", lambda e, pa=pa, w=w, t1=t1, gc=gc: e.tensor_scalar(
                                t1[:, 0:w], bank(pa, w), gc, 1.0, op0=ALU.mult, op1=ALU.mult),
                                reads=[("ps", pa), "gcols"], writes=["Ta2"])
                        add("dve", lambda e, w=w, t1=t1, rsq=rsq, dst=dst, fs=fin_scale: e.scalar_tensor_tensor(
                            dst, t1[:, 0:w], fs, rsq[:, 0:w], op0=ALU.mult, op1=ALU.mult),
                            reads=["Ta2", "Ta1"], writes=[dkey])
                for tc0 in range(0, NKC, 4):
                    nch = min(4, NKC - tc0)
                    qk["v"] += 1
                    pv = 6 + qk["v"] % 2
                    for a in range(nch):
                        tok0 = (tc0 + a) * 128
                        for kc in range(NCH):
                            add("pe", lambda e, a=a, kc=kc, tok0=tok0, pv=pv, wh=wh: e.matmul(
                                bank(pv, 128, a * 128), hv(kc, tok0, 128), wh[:, kc, 4, :],
                                start=(kc == 0), stop=(kc == NCH - 1)),
                                reads=[("wh", hsl), ("h", kc, tix(tok0))], writes=[("ps", pv)])
                    copy_alt(v_sb[:, tc0:tc0 + nch, :], bank(pv, nch * 128).rearrange("p (a n) -> p a n", a=nch),
                             reads=[("ps", pv)], writes=[("v", tc0 // 4)])
                npair = NKC // 2
                steps = [(qt, kp) for qt in range(NQT) for kp in range(npair)]
                r, on = T_a[4], T_a[5]
                GQ = min(4, NQT)
                ddg = carve(50 * KB, [128, 1024], F32)
                ssg = carve(54 * KB, [128, 1024], F32)
                KD, KS = ["Ta0", "Ta1"], ["Ta2", "Ta3"]

                def emit_qk(i):
                    qt, kp = steps[i]
                    q0 = qt * 256
                    sset = (qk["st"] + i) % 2
                    bA, bB = sset * 2, sset * 2 + 1
                    for kk in range(2):
                        kc = kp * 2 + kk
                        add("pe", lambda e, kc=kc, kk=kk, bA=bA, q0=q0: e.matmul(
                            bank(bA, 256, kk * 256), kT[0:64, kc * 128:(kc + 1) * 128], qT[0:64, q0:q0 + 256],
                            start=True, stop=True),
                            reads=[("kT", tix(kc * 128)), ("qT", tix(q0))], writes=[("ps", bA)])
                        add("pe", lambda e, kc=kc, kk=kk, bB=bB, q0=q0: e.matmul(
                            bank(bB, 256, kk * 256), kT[64:128, kc * 128:(kc + 1) * 128], qT[64:128, q0:q0 + 256],
                            start=True, stop=True),
                            reads=[("kT", tix(kc * 128)), ("qT", tix(q0))], writes=[("ps", bB)])

                def emit_exp_pv(i):
                    qt, kp = steps[i]
                    sset = (qk["st"] + i) % 2
                    bA, bB = sset * 2, sset * 2 + 1
                    Pb = P_s[sset]
                    ob, sb_ = (4, 5) if qt % 2 == 0 else (6, 7)
                    add("act", lambda e, bA=bA, Pb=Pb: e.activation(
                        Pb.rearrange("p k c q -> p c k q"),
                        psall[:, bA * 512:(bA + 2) * 512].rearrange("p (c k q) -> p c k q", c=2, k=2), AF.Exp),
                        reads=[("ps", bA), ("ps", bB)], writes=[("P", sset)])
                    for kk in range(2):
                        kc = kp * 2 + kk
                        add("pe", lambda e, kc=kc, kk=kk, ob=ob, Pb=Pb: e.matmul(
                            bank(ob), v_sb[:, kc, :], Pb[:, kk].rearrange("p c q -> p (c q)"),
                            start=(kc == 0), stop=(kc == NKC - 1)),
                            reads=[("P", sset), ("v", kc // 4)], writes=[("ps", ob)])
                        add("pe", lambda e, kc=kc, kk=kk, sb_=sb_, Pb=Pb: e.matmul(
                            bank(sb_), ones, Pb[:, kk].rearrange("p c q -> p (c q)"),
                            start=(kc == 0), stop=(kc == NKC - 1)),
                            reads=[("P", sset), "ones"], writes=[("ps", sb_)])

                def combine1(qt):
                    ob, sb_ = (4, 5) if qt % 2 == 0 else (6, 7)
                    g0 = (qt % GQ) * 256
                    add("dve", lambda e, sb_=sb_: e.reciprocal(r, bank(sb_)), reads=[("ps", sb_)], writes=["Ta4"])
                    add("dve", lambda e, ob=ob: e.tensor_tensor(on, bank(ob), r, op=ALU.mult),
                        reads=[("ps", ob), "Ta4"], writes=["Ta5"])
                    add("dve", lambda e, g0=g0: e.scalar_tensor_tensor(
                        ddg[:, g0:g0 + 256], on[:, 256:512], neglam, on[:, 0:256], op0=ALU.mult, op1=ALU.add),
                        reads=["Ta5", "neglam"], writes=KD)

                def combine_sq(qt):
                    g0 = (qt % GQ) * 256
                    add("act", lambda e, g0=g0: e.activation(sqa[:, 0:256], ddg[:, g0:g0 + 256], AF.Square),
                        reads=KD, writes=["sqa"])

                def combine2(qt, h=h):
                    ob, sb_ = (4, 5) if qt % 2 == 0 else (6, 7)
                    g0 = (qt % GQ) * 256
                    add("pe", lambda e, sb_=sb_: e.matmul(bank(sb_, 256), ones, sqa[:, 0:256], start=True, stop=True),
                        reads=["sqa", "ones"], writes=[("ps", sb_)])
                    add("dve", lambda e, sb_=sb_, g0=g0: e.tensor_copy(ssg[:, g0:g0 + 256], bank(sb_, 256)),
                        reads=[("ps", sb_)], writes=KS)
                    if qt % GQ == GQ - 1:
                        W = GQ * 256
                        qb = (qt - (GQ - 1)) * 256
                        add("act", lambda e, W=W: e.activation(ssg[:, 0:W], ssg[:, 0:W], AF.Sqrt, scale=1.0 / 128, bias=EPS),
                            reads=KS, writes=KS)
                        add("dve", lambda e, W=W: e.reciprocal(ssg[:, 0:W], ssg[:, 0:W]), reads=KS, writes=KS)
                        add("dve", lambda e, W=W, qb=qb, h=h: e.scalar_tensor_tensor(
                            onT[:, h % 2, qb:qb + W], ddg[:, 0:W], subg08, ssg[:, 0:W], op0=ALU.mult, op1=ALU.mult),
                            reads=KD + KS + ["subg08"], writes=[("onT", h % 2, qt - k) for k in range(GQ)])

                deferred = []
                emit_qk(0)
                for i in range(len(steps)):
                    qt, kp = steps[i]
                    if i + 1 < len(steps):
                        emit_qk(i + 1)
                    for dfr in list(deferred):
                        if dfr[0] <= i:
                            dfr[1](dfr[2])
                            deferred.remove(dfr)
                    emit_exp_pv(i)
                    if kp == npair - 1:
                        combine1(qt)
                        deferred.append((i + 4, combine_sq, qt))
                        deferred.append((i + 6, combine2, qt))
                for dfr in deferred:
                    dfr[1](dfr[2])
                qk["st"] += len(steps)
                if h % 2 == 1:
                    for (t0, w) in TILES[:NT]:
                        ti = tix(t0)
                        for d in range(NCH):
                            qk["cmb"] += 1
                            ob2 = qk["cmb"] % 4
                            for hh in range(2):
                                add("pe", lambda e, hh=hh, d=d, ob2=ob2, t0=t0, w=w: e.matmul(
                                    bank(ob2, w), wo_s[:, hh, d * 128:(d + 1) * 128], onT[:, hh, t0:t0 + w],
                                    start=(hh == 0), stop=(hh == 1)),
                                    reads=["wo", ("onT", hh, 2 * ti), ("onT", hh, 2 * ti + 1)], writes=[("ps", ob2)])
                            add("dve", lambda e, d=d, ob2=ob2, t0=t0, w=w: e.scalar_tensor_tensor(
                                xv(d, t0, w), bank(ob2, w), Gcol(l, 1, d, j_lat), xv(d, t0, w), op0=ALU.mult, op1=ALU.add),
                                reads=[("ps", ob2), ("x", d, ti), ("Gh", l, 1)], writes=[("x", d, ti)])

        def fourier(j_lat):
            l = 1
            NNT = SEQ // 128
            NKT = SEQ // 256
            add("sp", lambda e: e.dma_start(out=dftc_t.rearrange("p t c n -> p t (c n)"),
                                            in_=dftc_d.rearrange("t p c n -> p t (c n)")), writes=["dftc"], dma=True)
            fc = {"i": 0, "o": 0}
            for hf in range(2):
                add("pool", lambda e, hf=hf: e.dma_start(
                    out=wf_s.rearrange("p a n -> p (a n)"), in_=wf_d[hf].rearrange("p a n -> p (a n)")),
                    writes=["wf"], dma=True)
                for nt in range(NNT):
                    for tb in range(2):
                        pb = tb * 2 + nt % 2
                        for gq in range(2):
                            grp = 2 * hf + gq
                            for cc in range(2):
                                add("pe", lambda e, nt=nt, tb=tb, pb=pb, gq=gq, grp=grp, cc=cc: e.matmul(
                                    bank(pb, 256, gq * 256), hb[:, grp * 2 + cc, nt * 128:(nt + 1) * 128], dftc_t[:, tb, cc, :],
                                    start=(cc == 0), stop=(cc == 1)),
                                    reads=[("h", grp * 2 + cc, tix(nt * 128)), "dftc"], writes=[("ps", pb)])
                        U = U_c if tb == 0 else U_s
                        copy_alt(U[:, nt, :], bank(pb), reads=[("ps", pb)], writes=[("U", tb, nt)])
                for kt in range(NKT):
                    fc["i"] += 1
                    dsl = fc["i"] % 2
                    tabs = dft_s[dsl]
                    add("sp", lambda e, tabs=tabs, kt=kt: e.dma_start(
                        out=tabs.rearrange("p t a n -> p t (a n)"), in_=dftn_d[:, kt].rearrange("t p a n -> p t (a n)")),
                        writes=[("dft", dsl)], dma=True)
                    fT = fT_s[dsl]
                    for kcc in range(4):
                        pb = 4 + kcc % 2
                        for tb in range(2):
                            U = U_c if tb == 0 else U_s
                            for nt in range(NNT):
                                add("pe", lambda e, U=U, tb=tb, nt=nt, kcc=kcc, pb=pb, tabs=tabs: e.matmul(
                                    bank(pb, 256), U[:, nt, kcc * 128:(kcc + 1) * 128], tabs[:, tb, nt, :],
                                    start=(tb == 0 and nt == 0), stop=(tb == 1 and nt == NNT - 1)),
                                    reads=[("U", tb, nt), ("dft", dsl)], writes=[("ps", pb)])
                        copy_alt(fT[:, kcc, :], bank(pb, 256), reads=[("ps", pb)], writes=[("fT", dsl, kcc)])
                    t0 = kt * 256
                    ti = tix(t0)
                    for d in range(NCH):
                        fc["o"] += 1
                        ob = 6 + fc["o"] % 2
                        for kcc in range(4):
                            add("pe", lambda e, kcc=kcc, d=d, ob=ob, fT=fT: e.matmul(
                                bank(ob, 256), wf_s[:, kcc, d * 128:(d + 1) * 128], fT[:, kcc, :],
                                start=(kcc == 0), stop=(kcc == 3)),
                                reads=["wf", ("fT", dsl, kcc)], writes=[("ps", ob)])
                        add("dve", lambda e, d=d, ob=ob, t0=t0: e.scalar_tensor_tensor(
                            xs[:, d, t0:t0 + 256], bank(ob, 256), Gcol(l, 1, d, j_lat), xs[:, d, t0:t0 + 256],
                            op0=ALU.mult, op1=ALU.add),
                            reads=[("ps", ob), ("x", d, ti), ("Gh", l, 1)], writes=[("x", d, ti)])

        LAT = TILES[:NT]
        for bi in range(NB):
            j = bi
            load_stream(bi)
            norm_mod(0, 0, j, TILES)
            ffn(0, 0, j, TILES, next_norm=(0, 1), extra=(EXTRA[(0, 0)] if bi == 0 else ()))
            S.barrier()
            attention(j)
            S.barrier()
            norm_mod(0, 2, j, LAT)
            ffn(0, 1, j, LAT, next_norm=(1, 0), extra=(EXTRA[(0, 1)] if bi == 0 else ()))
            ffn(1, 0, j, LAT, next_norm=(1, 1))
            S.barrier()
            fourier(j)
            S.barrier()
            norm_mod(1, 2, j, LAT)
            ffn(1, 1, j, LAT)
            store_stream(bi)
        outkeys = [k for k in S.last_writer if isinstance(k, tuple) and k[0] == "out"]
        add("sp", lambda e: e.nop(), reads=outkeys)

        S.finalize()
        with nc.Block() as block:
            @block.sync
            def _(e):
                S.emit(sems, "sp", e)

            @block.tensor
            def _(e):
                S.emit(sems, "pe", e)

            @block.scalar
            def _(e):
                S.emit(sems, "act", e)

            @block.vector
            def _(e):
                S.emit(sems, "dve", e)

            @block.gpsimd
            def _(e):
                S.emit(sems, "pool", e)
    return nc


def make_shared_inputs(SEQ, DFF, inp):
    NG = DFF // 256
    f = np.float32
    sh = {}
    sh["gT"] = np.ascontiguousarray(np.asarray(inp["norm_g"], f).reshape(2, 3, NCH, 128).transpose(3, 0, 1, 2))
    sh["bT"] = np.ascontiguousarray(np.asarray(inp["b_mod"], f).reshape(2, 72, 128).transpose(2, 0, 1))
    wm = np.asarray(inp["w_mod"], f).reshape(2, NCH, 128, 9, 1024)
    sh["wmod"] = np.ascontiguousarray(wm.transpose(0, 3, 2, 1, 4))
    wgu = np.stack([np.asarray(inp["ffn1_w_gu"], f), np.asarray(inp["ffn2_w_gu"], f)], axis=1)
    wgu = wgu.reshape(2, 2, NCH, 128, 2, NG, 256)
    sh["wgu"] = np.ascontiguousarray(wgu.transpose(0, 1, 5, 3, 2, 4, 6)).reshape(2, 2, NG, 128, NCH, 512)
    wd = np.stack([np.asarray(inp["ffn1_w_d"], f), np.asarray(inp["ffn2_w_d"], f)], axis=1)
    wd = wd.reshape(2, 2, NG, 2, 128, 1024)
    sh["wd"] = np.ascontiguousarray(wd.transpose(0, 1, 2, 4, 3, 5))
    wqkv = np.asarray(inp["attn_w_qkv"], f)[0]
    perm = _partner_perm()
    wq = wqkv[:, 0:1024].reshape(NCH, 128, NH, 128)
    wk = wqkv[:, 1024:2048].reshape(NCH, 128, NH, 128)
    wv = wqkv[:, 2048:3072].reshape(NCH, 128, NH, 128)
    wa = np.stack([wq, wq[..., perm], wk, wk[..., perm], wv], axis=3)
    sh["wattn"] = np.ascontiguousarray(wa.transpose(2, 1, 0, 3, 4))
    wo = np.asarray(inp["attn_w_o"], f)[0].reshape(NH // 2, 2, 128, 1024)
    sh["wo"] = np.ascontiguousarray(wo.transpose(0, 2, 1, 3))
    wf = np.asarray(inp["fourier_w"], f)[0].reshape(2, 4, 128, 1024)
    sh["wf"] = np.ascontiguousarray(wf.transpose(0, 2, 1, 3))
    qg = np.asarray(inp["attn_q_g"], f)[0]
    kg = np.asarray(inp["attn_k_g"], f)[0]
    sg = np.asarray(inp["attn_sub_g"], f)[0]
    qg2, kg2 = np.tile(qg, 2), np.tile(kg, 2)
    sh["gcols"] = np.ascontiguousarray(np.stack([qg2, qg2[perm], kg2, kg2[perm], sg], axis=1))
    sh["lamv"] = np.ascontiguousarray(np.stack([np.asarray(inp[k], f)[0] for k in
                                                ("attn_lam_q1", "attn_lam_k1", "attn_lam_q2", "attn_lam_k2")]))
    sh["ident"] = np.eye(128, dtype=f)
    bo = np.zeros((128, 128), f)
    bo[:64, :64] = 1.0
    bo[64:, 64:] = 1.0
    sh["bones"] = bo
    sh["ropecos"], sh["ropesin"] = _rope_tables(SEQ)
    sh["dftn"], sh["dftc"] = _dft_tables(SEQ)
    return sh


def make_in_maps(SEQ, DFF, NB, ncores, inp):
    sh = make_shared_inputs(SEQ, DFF, inp)
    x = np.asarray(inp["x"], np.float32)
    ctx = np.asarray(inp["ctx"], np.float32)
    c = np.asarray(inp["c"], np.float32)
    c_ctx = np.asarray(inp["c_ctx"], np.float32)
    maps = []
    for i in range(ncores):
        b0 = i * NB
        conds = [c[b0 + min(k, NB - 1)] for k in range(2)] + [c_ctx]
        cond = np.stack(conds)
        condT = np.ascontiguousarray(cond.reshape(3, NCH, 128).transpose(2, 1, 0))
        m = dict(sh)
        m["x"] = np.ascontiguousarray(x[b0:b0 + NB])
        m["ctx"] = np.ascontiguousarray(ctx[b0:b0 + NB])
        m["condT"] = condT
        maps.append(m)
    return maps


_CACHE = {}


def kernel(x, c, ctx, c_ctx, norm_g, w_mod, b_mod, ffn1_w_gu, ffn1_w_d, ffn2_w_gu, ffn2_w_d,
           attn_w_qkv, attn_w_o, attn_q_g, attn_k_g, attn_lam_q1, attn_lam_k1, attn_lam_q2,
           attn_lam_k2, attn_sub_g, fourier_w):
    inp = dict(x=x, c=c, ctx=ctx, c_ctx=c_ctx, norm_g=norm_g, w_mod=w_mod, b_mod=b_mod,
               ffn1_w_gu=ffn1_w_gu, ffn1_w_d=ffn1_w_d, ffn2_w_gu=ffn2_w_gu, ffn2_w_d=ffn2_w_d,
               attn_w_qkv=attn_w_qkv, attn_w_o=attn_w_o, attn_q_g=attn_q_g, attn_k_g=attn_k_g,
               attn_lam_q1=attn_lam_q1, attn_lam_k1=attn_lam_k1, attn_lam_q2=attn_lam_q2,
               attn_lam_k2=attn_lam_k2, attn_sub_g=attn_sub_g, fourier_w=fourier_w)
    B, SEQ, _ = np.shape(x)
    DFF = np.shape(ffn1_w_d)[1]
    ncores = 8
    NB = B // ncores
    nc = build_program(SEQ, DFF, NB)
    maps = make_in_maps(SEQ, DFF, NB, ncores, inp)
    res = run_bass_kernel_spmd(nc, maps, core_ids=list(range(ncores)))
    out = np.concatenate([np.asarray(r["out"], np.float32) for r in res.results], axis=0)
    return out
```
